# Optimizing a Trainium2 kernel written in Bass

```python
import math
import jax, jax.numpy as jnp
from jax import lax
import numpy as np

D_MODEL = 1024
BATCH = 16
SEQ = 256
DEPTH = 4
DEC_BATCH = 2
DEC_SEQ = 1024
PAST_LEN = 512

GRID_W = 64
D_MIX = D_MODEL
H_A = 4
DA = 64
H_B = 4
DB = 64
NA_ROWS = 8
NA_COLS = 16
NA_QCB = 16
NA_KCB = NA_QCB + NA_COLS
H_C = 4
DC = 64
CHUNK = 64
D_FF = 4 * D_MODEL
QBLK = 128
ROPE_BASE = 10000.0
EPS = 1e-5
ALPHA = (2 * DEPTH) ** 0.25
BETA = (8 * DEPTH) ** -0.25
A_QK = H_A * 2 * DA
A_V = H_A * 2 * DA
B_W = H_B * DB
C_W = H_C * DC
C_GATES = 4 * H_C
IN_SIZES = (A_QK, A_QK, A_V, B_W, B_W, B_W, C_W, C_W, C_W, C_W, C_GATES)
N_IN = A_QK * 2 + A_V + 3 * B_W + 4 * C_W + C_GATES

kernel_name = "hybrid_diffusion_paraheads_step"


def layer_norm(x, g, b):
    xf = x.astype(jnp.float32)
    mu = jnp.mean(xf, -1, keepdims=True)
    var = jnp.mean(jnp.square(xf - mu), -1, keepdims=True)
    return ((xf - mu) * lax.rsqrt(var + EPS) * g.astype(jnp.float32) + b.astype(jnp.float32)).astype(x.dtype)


def head_rms(x, g):
    xf = x.astype(jnp.float32)
    return xf * lax.rsqrt(jnp.mean(xf * xf, -1, keepdims=True) + EPS) * g.astype(jnp.float32)


def merge_heads(x):
    b, h, n, d = x.shape
    return x.transpose(0, 2, 1, 3).reshape(b, n, h * d)


def split_proj(p, gate_bias):
    b, n, _ = p.shape
    offs = np.cumsum(IN_SIZES)[:-1].tolist()
    aq, ak, av, bq, bk, bv, cq, ck, cv, co, cg = jnp.split(p, offs, axis=-1)
    two = lambda t: t.reshape(b, n, H_A, 2, DA).transpose(0, 2, 1, 3, 4)
    heads = lambda t, h: t.reshape(b, n, h, -1).transpose(0, 2, 1, 3)
    gates = (cg + gate_bias).astype(jnp.float32).reshape(b, n, 4, H_C).transpose(0, 2, 3, 1)
    return (two(aq), two(ak), heads(av, H_A), heads(bq, H_B), heads(bk, H_B), heads(bv, H_B),
            heads(cq, H_C), heads(ck, H_C), heads(cv, H_C), heads(co, H_C), gates)


def map_query_blocks(fn, q):
    b, h, nq = q.shape[:3]
    nb = nq // QBLK
    qb = jnp.moveaxis(q.reshape(b, h, nb, QBLK, *q.shape[3:]), 2, 0)
    out = lax.map(fn, qb)
    return jnp.moveaxis(out, 0, 2).reshape(b, h, nq, -1)


def rope_1d(x, cos, sin):
    half = x.shape[-1] // 2
    x1, x2 = x[..., :half], x[..., half:]
    return jnp.concatenate([x1 * cos - x2 * sin, x1 * sin + x2 * cos], axis=-1)


def axial_rope(x):
    n = x.shape[2]
    dh = DA // 2
    nf = dh // 2
    freqs = ROPE_BASE ** (-jnp.arange(nf, dtype=jnp.float32) / nf)
    t = jnp.arange(n)
    row = (t // GRID_W).astype(jnp.float32)
    col = (t % GRID_W).astype(jnp.float32)
    ar = (row[:, None] * freqs).reshape(n, 1, nf)
    ac = (col[:, None] * freqs).reshape(n, 1, nf)
    xf = x.astype(jnp.float32)
    out = jnp.concatenate([rope_1d(xf[..., :dh], jnp.cos(ar), jnp.sin(ar)),
                           rope_1d(xf[..., dh:], jnp.cos(ac), jnp.sin(ac))], axis=-1)
    return out.astype(x.dtype)


def diff_lambda_value(lp, lam_init):
    lp = lp.astype(jnp.float32)
    return jnp.exp(jnp.sum(lp[0] * lp[1])) - jnp.exp(jnp.sum(lp[2] * lp[3])) + lam_init


def diff_attention(q, k, v, lam, g, lam_init):
    scale = DA ** -0.5

    def block(qb):
        s = jnp.einsum('bhqmd,bhkmd->bhmqk', qb, k).astype(jnp.float32) * scale
        p = jax.nn.softmax(s, axis=-1)
        a = p[:, :, 0] - lam * p[:, :, 1]
        return jnp.einsum('bhqk,bhkv->bhqv', a.astype(v.dtype), v)

    o = map_query_blocks(block, q)
    return (head_rms(o, g) * (1.0 - lam_init)).astype(v.dtype)


def dense_attention(q, k, v):
    scale = q.shape[-1] ** -0.5

    def block(qb):
        s = jnp.einsum('bhqd,bhkd->bhqk', qb, k).astype(jnp.float32) * scale
        p = jax.nn.softmax(s, axis=-1)
        return jnp.einsum('bhqk,bhkd->bhqd', p.astype(v.dtype), v)

    return map_query_blocks(block, q)


def neighbourhood_attention(q, k, v, ctx_k, ctx_v, rpb):
    b, h, n, d = q.shape
    rows = n // GRID_W
    kh = min(NA_ROWS, rows)
    ncb = GRID_W // NA_QCB
    r = jnp.arange(rows)
    key_rows = jnp.clip(r - kh // 2, 0, rows - kh)[:, None] + jnp.arange(kh)[None, :]
    qcol = jnp.arange(GRID_W).reshape(ncb, NA_QCB)
    win_start = jnp.clip(qcol - NA_COLS // 2, 0, GRID_W - NA_COLS)
    blk_start = jnp.clip(jnp.arange(ncb) * NA_QCB - NA_COLS // 2, 0, GRID_W - NA_KCB)
    key_cols = blk_start[:, None] + jnp.arange(NA_KCB)[None, :]
    kc = key_cols[:, None, :]
    valid = (kc >= win_start[..., None]) & (kc < win_start[..., None] + NA_COLS)
    dr = key_rows - r[:, None] + NA_ROWS - 1
    dc = kc - qcol[..., None] + NA_COLS - 1
    bias = jnp.take(jnp.take(rpb, dr, axis=1, mode='clip'), dc, axis=3, mode='clip').astype(jnp.float32)
    bias = jnp.where(valid, bias, -jnp.inf).transpose(0, 1, 3, 4, 2, 5)
    qg = q.reshape(b, h, rows, ncb, NA_QCB, d)
    gather = lambda t: jnp.take(jnp.take(t.reshape(b, h, rows, GRID_W, d), key_rows, axis=2), key_cols, axis=4)
    kg = gather(k)
    vg = gather(v)
    scale = d ** -0.5
    s_loc = jnp.einsum('bhrjcd,bhrijkd->bhrjcik', qg, kg).astype(jnp.float32) * scale + bias[None]
    nloc = kh * NA_KCB
    s_loc = s_loc.reshape(b, h, rows, ncb, NA_QCB, nloc)
    s_ctx = jnp.einsum('bhrjcd,bhld->bhrjcl', qg, ctx_k).astype(jnp.float32) * scale
    p = jax.nn.softmax(jnp.concatenate([s_loc, s_ctx], axis=-1), axis=-1).astype(v.dtype)
    p_loc = p[..., :nloc].reshape(b, h, rows, ncb, NA_QCB, kh, NA_KCB)
    o = (jnp.einsum('bhrjcik,bhrijkd->bhrjcd', p_loc, vg)
         + jnp.einsum('bhrjcl,bhld->bhrjcd', p[..., nloc:], ctx_v))
    return o.reshape(b, h, n, d)


def mlstm_chunked(q, k, v, ig, lf, c0, n0, m0):
    b, h, n, _ = q.shape
    nc = n // CHUNK
    chunks = lambda t: jnp.moveaxis(t.reshape(b, h, nc, CHUNK, *t.shape[3:]), 2, 0)
    tri = jnp.tril(jnp.ones((CHUNK, CHUNK), bool))

    def step(carry, xs):
        cm, nm, m = carry
        qc, kc, vc, ic, fc = xs
        bcum = jnp.cumsum(fc, axis=-1)
        dmat = jnp.where(tri, bcum[..., :, None] - bcum[..., None, :] + ic[..., None, :], -jnp.inf)
        inter = bcum + m[..., None]
        m_t = jnp.maximum(inter, jnp.max(dmat, -1))
        s = jnp.einsum('bhtd,bhsd->bhts', qc, kc) * jnp.exp(dmat - m_t[..., None])
        e = jnp.exp(inter - m_t)
        num = e[..., None] * jnp.einsum('bhtd,bhdv->bhtv', qc, cm) + jnp.einsum('bhts,bhsv->bhtv', s, vc)
        den = e * jnp.einsum('bhtd,bhd->bht', qc, nm) + jnp.sum(s, -1)
        hout = num / jnp.maximum(jnp.abs(den), jnp.exp(-m_t))[..., None]
        b_tot = bcum[..., -1]
        g = b_tot[..., None] - bcum + ic
        m_new = jnp.maximum(b_tot + m, jnp.max(g, -1))
        decay = jnp.exp(b_tot + m - m_new)
        wk = jnp.exp(g - m_new[..., None])
        c_new = decay[..., None, None] * cm + jnp.einsum('bhs,bhsd,bhsv->bhdv', wk, kc, vc)
        n_new = decay[..., None] * nm + jnp.einsum('bhs,bhsd->bhd', wk, kc)
        return (c_new, n_new, m_new), hout

    (cf, nf, mf), hs = lax.scan(step, (c0, n0, m0), tuple(chunks(t) for t in (q, k, v, ig, lf)))
    return jnp.moveaxis(hs, 0, 2).reshape(b, h, n, -1), cf, nf, mf


def mlstm_mix(q, k, v, o, gates, g, c0, n0, m0):
    f32 = jnp.float32
    qf = q.astype(f32) * DC ** -0.5
    kf, vf = k.astype(f32), v.astype(f32)
    c0, n0, m0 = c0.astype(f32), n0.astype(f32), m0.astype(f32)
    i_fw, f_fw, i_bw, f_bw = gates[:, 0], gates[:, 1], gates[:, 2], gates[:, 3]
    h_f, cf, nf, mf = mlstm_chunked(qf, kf, vf, i_fw, jax.nn.log_sigmoid(f_fw), c0[:, 0], n0[:, 0], m0[:, 0])
    flip = lambda t: jnp.flip(t, axis=2)
    h_b, cb, nb, mb = mlstm_chunked(flip(qf), flip(kf), flip(vf), flip(i_bw), flip(jax.nn.log_sigmoid(f_bw)),
                                    c0[:, 1], n0[:, 1], m0[:, 1])
    hsum = h_f + flip(h_b)
    out = head_rms(hsum, g[:, None, :]) * jax.nn.sigmoid(o.astype(f32))
    return (out.astype(q.dtype), jnp.stack([cf, cb], 1), jnp.stack([nf, nb], 1), jnp.stack([mf, mb], 1))


def adaln(cond, w, b):
    return jnp.split(jax.nn.silu(cond) @ w + b, 6, axis=-1)


def channel_mix(x, shift, scale, gate, w1, w2, g, b):
    hmod = x * (1 + scale) + shift
    y = jnp.square(jax.nn.relu(hmod @ w1)) @ w2
    return layer_norm(ALPHA * x + gate * y, g, b)


def context_mix(h, wi, gb, lam, lam_init, dg, mg):
    b, l, _ = h.shape
    aq, ak, av, bq, bk, bv, cq, ck, cv, co, gates = split_proj(h @ wi, gb)
    oa = diff_attention(aq, ak, av, lam, dg, lam_init)
    ob = dense_attention(bq, bk, bv)
    zc = jnp.zeros((b, 2, H_C, DC, DC), jnp.float32)
    zn = jnp.zeros((b, 2, H_C, DC), jnp.float32)
    zm = jnp.zeros((b, 2, H_C), jnp.float32)
    oc, sc, sn, sm = mlstm_mix(cq, ck, cv, co, gates, mg, zc, zn, zm)
    mix = jnp.concatenate([merge_heads(oa), merge_heads(ob), merge_heads(oc)], -1).astype(h.dtype)
    return mix, ak.reshape(b, H_A, l, 2 * DA), av, bk, bv, sc, sn, sm


def latent_mix(h, wi, gb, lam, lam_init, dg, mg, rpb, ca_k, ca_v, cb_k, cb_v, s_c, s_n, s_m):
    b = h.shape[0]
    aq, ak, av, bq, bk, bv, cq, ck, cv, co, gates = split_proj(h @ wi, gb)
    lc = ca_k.shape[2]
    ka = jnp.concatenate([ca_k.reshape(b, H_A, lc, 2, DA).astype(ak.dtype), axial_rope(ak)], axis=2)
    va = jnp.concatenate([ca_v.astype(av.dtype), av], axis=2)
    oa = diff_attention(axial_rope(aq), ka, va, lam, dg, lam_init)
    ob = neighbourhood_attention(bq, bk, bv, cb_k.astype(bk.dtype), cb_v.astype(bv.dtype), rpb)
    oc, _, _, _ = mlstm_mix(cq, ck, cv, co, gates, mg, s_c, s_n, s_m)
    return jnp.concatenate([merge_heads(oa), merge_heads(ob), merge_heads(oc)], -1).astype(h.dtype)


def setup_inputs(seed: int = 0) -> dict:
    key = jax.random.key(seed)
    ks = jax.random.split(key, 32)
    nrm = lambda i, shape, s: jax.random.normal(ks[i], shape, jnp.float32) * s
    gate_i = nrm(20, (DEPTH, 2, 1, H_C), 0.1)
    gate_f = jnp.linspace(3.0, 6.0, H_C, dtype=jnp.float32)[None, None, None, :] + nrm(21, (DEPTH, 2, 1, H_C), 0.1)
    mlstm_gate_bias = jnp.concatenate([gate_i, gate_f], axis=2).reshape(DEPTH, C_GATES)
    return {
        'x_prompt': nrm(0, (BATCH, SEQ, D_MODEL), 1.0),
        'x_sample': nrm(1, (DEC_BATCH, DEC_SEQ, D_MODEL), 1.0),
        'cache_a_k': nrm(2, (DEC_BATCH, DEPTH, H_A, PAST_LEN, 2 * DA), 1.0),
        'cache_a_v': nrm(3, (DEC_BATCH, DEPTH, H_A, PAST_LEN, 2 * DA), 1.0),
        'cache_b_k': nrm(4, (DEC_BATCH, DEPTH, H_B, PAST_LEN, DB), 1.0),
        'cache_b_v': nrm(5, (DEC_BATCH, DEPTH, H_B, PAST_LEN, DB), 1.0),
        'state_c': nrm(6, (DEC_BATCH, DEPTH, 2, H_C, DC, DC), 0.1),
        'state_n': nrm(7, (DEC_BATCH, DEPTH, 2, H_C, DC), 1.0),
        'state_m': nrm(8, (DEC_BATCH, DEPTH, 2, H_C), 0.5),
        'c': nrm(9, (DEC_BATCH, D_MODEL), 1.0),
        'c_ctx': nrm(10, (D_MODEL,), 1.0),
        'w_in': nrm(11, (DEPTH, D_MODEL, N_IN), D_MODEL ** -0.5),
        'mlstm_gate_bias': mlstm_gate_bias,
        'diff_lambda': nrm(12, (DEPTH, 4, DA), 0.1),
        'diff_norm_g': 1.0 + nrm(13, (DEPTH, 2 * DA), 0.02),
        'nat_rpb': nrm(14, (DEPTH, H_B, 2 * NA_ROWS - 1, 2 * NA_COLS - 1), 0.1),
        'mlstm_norm_g': 1.0 + nrm(15, (DEPTH, H_C, DC), 0.02),
        'w_out': nrm(16, (DEPTH, D_MIX, D_MODEL), BETA * D_MIX ** -0.5),
        'ada_w': nrm(17, (DEPTH, D_MODEL, 6 * D_MODEL), D_MODEL ** -0.5),
        'ada_b': nrm(18, (DEPTH, 6 * D_MODEL), 0.02),
        'ln1_g': 1.0 + nrm(19, (DEPTH, D_MODEL), 0.02),
        'ln1_b': nrm(22, (DEPTH, D_MODEL), 0.02),
        'ln2_g': 1.0 + nrm(23, (DEPTH, D_MODEL), 0.02),
        'ln2_b': nrm(24, (DEPTH, D_MODEL), 0.02),
        'w_mlp1': nrm(25, (DEPTH, D_MODEL, D_FF), D_MODEL ** -0.5),
        'w_mlp2': nrm(26, (DEPTH, D_FF, D_MODEL), BETA * D_FF ** -0.5),
    }


def reference(x_prompt, x_sample, cache_a_k, cache_a_v, cache_b_k, cache_b_v, state_c, state_n, state_m,
              c, c_ctx, w_in, mlstm_gate_bias, diff_lambda, diff_norm_g, nat_rpb, mlstm_norm_g, w_out,
              ada_w, ada_b, ln1_g, ln1_b, ln2_g, ln2_b, w_mlp1, w_mlp2):
    xp = x_prompt
    xs = x_sample
    ak_l, av_l, bk_l, bv_l, sc_l, sn_l, sm_l = [], [], [], [], [], [], []
    for l in range(DEPTH):
        lam_init = 0.8 - 0.6 * math.exp(-0.3 * l)
        lam = diff_lambda_value(diff_lambda[l], lam_init)
        sh1, sc1, g1, sh2, sc2, g2 = adaln(c_ctx, ada_w[l], ada_b[l])
        mix, ak, av, bk, bv, sc, sn, sm = context_mix(xp * (1 + sc1) + sh1, w_in[l], mlstm_gate_bias[l], lam,
                                                      lam_init, diff_norm_g[l], mlstm_norm_g[l])
        xp = layer_norm(ALPHA * xp + g1 * (mix @ w_out[l]), ln1_g[l], ln1_b[l])
        xp = channel_mix(xp, sh2, sc2, g2, w_mlp1[l], w_mlp2[l], ln2_g[l], ln2_b[l])
        ak_l.append(ak); av_l.append(av); bk_l.append(bk); bv_l.append(bv)
        sc_l.append(sc); sn_l.append(sn); sm_l.append(sm)
        sh1, sc1, g1, sh2, sc2, g2 = [t[:, None, :] for t in adaln(c, ada_w[l], ada_b[l])]
        mix = latent_mix(xs * (1 + sc1) + sh1, w_in[l], mlstm_gate_bias[l], lam, lam_init, diff_norm_g[l],
                         mlstm_norm_g[l], nat_rpb[l], cache_a_k[:, l], cache_a_v[:, l], cache_b_k[:, l],
                         cache_b_v[:, l], state_c[:, l], state_n[:, l], state_m[:, l])
        xs = layer_norm(ALPHA * xs + g1 * (mix @ w_out[l]), ln1_g[l], ln1_b[l])
        xs = channel_mix(xs, sh2, sc2, g2, w_mlp1[l], w_mlp2[l], ln2_g[l], ln2_b[l])
    new_a_k = jnp.stack(ak_l, 1)
    new_a_v = jnp.stack(av_l, 1)
    new_b_k = jnp.stack(bk_l, 1)
    new_b_v = jnp.stack(bv_l, 1)
    new_c = jnp.stack(sc_l, 1)
    new_n = jnp.stack(sn_l, 1)
    new_m = jnp.stack(sm_l, 1)
    return (xp, xs, new_a_k, new_a_v, new_b_k, new_b_v, new_c, new_n, new_m)
```

```python
import contextlib
import math
import numpy as np
import concourse.bass as bass
import concourse.mybir as mybir
from concourse.bass_utils import run_bass_kernel_spmd

F32 = mybir.dt.float32
BF16 = mybir.dt.bfloat16
AF = mybir.ActivationFunctionType
ALU = mybir.AluOpType
AX = mybir.AxisListType
_DT_SIZE = {F32: 4, BF16: 2}

D_MODEL = 1024
BATCH = 16
SEQ = 256
DEPTH = 4
DEC_SEQ = 1024
PAST = 512
N_IN = 3344
D_FF = 4096
EPS = 1e-5
ALPHA = (2 * DEPTH) ** 0.25
NEG = -30000.0
NT = 1536


class _Op:
    __slots__ = ("eng", "fn", "deps", "signal", "idx_in_eng", "is_dma", "dma_slot", "dma_val", "sig_val")

    def __init__(self, eng, fn, is_dma):
        self.eng = eng
        self.fn = fn
        self.deps = set()
        self.signal = False
        self.is_dma = is_dma
        self.dma_slot = None
        self.dma_val = None
        self.sig_val = None
        self.idx_in_eng = 0


class Prog:
    ENGS = ("pe", "act", "dve", "pool", "sp")
    NDMA = 8

    def __init__(self, nc):
        self.nc = nc
        self.ops = []
        self.eng_ops = {e: [] for e in self.ENGS}
        self.hist = {}
        self.tinfo = {}
        self.stack = contextlib.ExitStack()
        self.ro = set()
        self.bank_rr = 0
        self.skip = False

    def sbuf(self, name, shape, dtype):
        t = self.stack.enter_context(self.nc.sbuf_tensor(name, list(shape), dtype))
        self.tinfo[t.name] = ("SB", int(np.prod(shape[1:])), _DT_SIZE[dtype])
        return t

    def psum_all(self):
        t = self.stack.enter_context(self.nc.psum_tensor("psall", [128, 4096], F32))
        self.tinfo[t.name] = ("PSUM", 4096, 4)
        self.ps = t
        self.psb = t.bitcast(BF16)
        return t

    def reg_dram(self, t, readonly):
        self.tinfo[t.name] = ("DRAM", 1, 1)
        if readonly:
            self.ro.add(t.name)

    def bank(self, n=1):
        if self.bank_rr + n > 8:
            self.bank_rr = 0
        b = self.bank_rr
        self.bank_rr = (self.bank_rr + n) % 8
        return b

    def _boxes(self, ap):
        name = ap.tensor.name
        info = self.tinfo[name]
        space, rowsize, base_dts = info
        if space == "DRAM":
            if name in self.ro:
                return []
            lo = hi = int(ap.offset)
            for st, cnt in ap.ap:
                ext = st * (cnt - 1)
                if ext < 0:
                    lo += ext
                else:
                    hi += ext
            return [(("D", name), (0, 1, lo, hi + 1))]
        dts = _DT_SIZE[ap.dtype]
        off = int(ap.offset)
        pat = list(ap.ap)
        pstep, pcnt = pat[0]
        rs = rowsize * base_dts // dts
        p0 = off // rs
        st0 = off % rs
        lo = hi = st0
        for st, cnt in pat[1:]:
            ext = st * (cnt - 1)
            if ext < 0:
                lo += ext
            else:
                hi += ext
        b0, b1 = lo * dts, (hi + 1) * dts
        if space == "PSUM":
            return [(("P", bk), (0, 128, 0, 2048)) for bk in range(b0 // 2048, (b1 - 1) // 2048 + 1)]
        return [(("S", name), (p0, p0 + pcnt, b0, b1))]

    @staticmethod
    def _ovl(a, b):
        return a[0] < b[1] and b[0] < a[1] and a[2] < b[3] and b[2] < a[3]

    @staticmethod
    def _covers(a, b):
        return a[0] <= b[0] and a[1] >= b[1] and a[2] <= b[2] and a[3] >= b[3]

    def add(self, eng, fn, reads=(), writes=(), is_dma=False):
        if self.skip:
            return -1
        op = _Op(eng, fn, is_dma)
        idx = len(self.ops)
        self.ops.append(op)
        op.idx_in_eng = len(self.eng_ops[eng])
        self.eng_ops[eng].append(idx)
        accs = []
        for ap in reads:
            if ap is None:
                continue
            for key, box in self._boxes(ap):
                accs.append((key, box, key[0] == "P"))
        for ap in writes:
            if ap is None:
                continue
            for key, box in self._boxes(ap):
                accs.append((key, box, True))
        for key, box, isw in accs:
            h = self.hist.setdefault(key, [])
            for (obox, oidx, oisw) in h:
                if oidx != idx and (isw or oisw) and self._ovl(box, obox):
                    op.deps.add(oidx)
        for key, box, isw in accs:
            h = self.hist[key]
            if isw:
                h[:] = [r for r in h if not (self._covers(box, r[0]) and r[1] != idx)]
            h.append((box, idx, isw))
        return idx

    @staticmethod
    def _isap(v):
        return isinstance(v, bass.AP)

    def mm(self, out, lhsT, rhs, start=True, stop=True):
        return self.add("pe", lambda e: e.matmul(out, lhsT, rhs, start=start, stop=stop), [lhsT, rhs], [out])

    def transpose(self, out, in_, ident):
        return self.add("pe", lambda e: e.transpose(out, in_, ident), [in_, ident], [out])

    def act(self, out, in_, func, bias=None, scale=None, accum_out=None):
        kw = {}
        reads = [in_]
        if bias is not None:
            kw["bias"] = bias
            if self._isap(bias):
                reads.append(bias)
        if scale is not None:
            kw["scale"] = scale
            if self._isap(scale):
                reads.append(scale)
        if accum_out is not None:
            kw["accum_out"] = accum_out
        return self.add("act", lambda e: e.activation(out, in_, func, **kw), reads, [out, accum_out])

    def tt(self, out, in0, in1, op, eng="dve"):
        return self.add(eng, lambda e: e.tensor_tensor(out, in0, in1, op), [in0, in1], [out])

    def ts(self, out, in0, s1, s2=None, op0=ALU.mult, op1=None, eng="dve"):
        reads = [in0] + [s for s in (s1, s2) if self._isap(s)]
        if op1 is None:
            return self.add(eng, lambda e: e.tensor_scalar(out, in0, s1, None, op0), reads, [out])
        return self.add(eng, lambda e: e.tensor_scalar(out, in0, s1, s2, op0, op1), reads, [out])

    def stt(self, out, in0, scalar, in1, op0, op1):
        reads = [in0, in1] + ([scalar] if self._isap(scalar) else [])
        return self.add("dve", lambda e: e.scalar_tensor_tensor(out, in0, scalar, in1, op0, op1), reads, [out])

    def copy(self, out, in_, eng="dve"):
        if eng == "act":
            return self.add("act", lambda e: e.copy(out, in_), [in_], [out])
        return self.add(eng, lambda e: e.tensor_copy(out, in_), [in_], [out])

    def reduce(self, out, in_, op, axis=None):
        ax = axis if axis is not None else AX.X
        return self.add("dve", lambda e: e.tensor_reduce(out, in_, ax, op), [in_], [out])

    def memset(self, ap, val, eng="dve"):
        return self.add(eng, lambda e: e.memset(ap, val), [], [ap])

    def recip(self, out, in_, lowp=False):
        if lowp:
            def fn(e):
                with self.nc.allow_low_precision("bf16 gate output"):
                    return e.reciprocal(out, in_)
            return self.add("dve", fn, [in_], [out])
        return self.add("dve", lambda e: e.reciprocal(out, in_), [in_], [out])

    def scan(self, out, d0, d1, init, op0, op1):
        reads = [d0, d1] + ([init] if self._isap(init) else [])
        return self.add("dve", lambda e: e.tensor_tensor_scan(out, d0, d1, init, op0, op1), reads, [out])

    def dma(self, q, out, in_):
        return self.add(q, lambda e: e.dma_start(out, in_, allow_slow_non_contiguous=True), [in_], [out], is_dma=True)

    def _need_sync(self, p, op):
        if p.eng == op.eng and not op.is_dma and p.eng == "pe":
            return False
        return True

    def emit(self):
        nc = self.nc
        ops = self.ops
        dcount = {"sp": 0, "pool": 0}
        for op in ops:
            if op.is_dma:
                n = dcount[op.eng]
                op.dma_slot = (op.eng, n % self.NDMA)
                op.dma_val = 16 * (n // self.NDMA + 1)
                dcount[op.eng] = n + 1
        for op in ops:
            for d in op.deps:
                p = ops[d]
                if p.is_dma:
                    continue
                if self._need_sync(p, op):
                    p.signal = True
        cnt = {e: 0 for e in self.ENGS}
        for op in ops:
            if op.signal:
                cnt[op.eng] += 1
                op.sig_val = cnt[op.eng]
        sems = {}
        for e in ("pe", "act", "dve", "pool"):
            sems[("c", e)] = self.stack.enter_context(nc.semaphore("s_" + e))
        for q in ("sp", "pool"):
            for k in range(self.NDMA):
                sems[("d", q, k)] = self.stack.enter_context(nc.semaphore("d_%s_%d" % (q, k)))
        final_dma = {}
        for op in ops:
            if op.is_dma:
                final_dma[op.dma_slot] = op.dma_val
        block = self.stack.enter_context(nc.Block())
        prog = self

        def body_for(engname):
            def body(e):
                waited = {}
                for oi in prog.eng_ops[engname]:
                    op = ops[oi]
                    need = {}
                    for d in op.deps:
                        p = ops[d]
                        if p.is_dma:
                            key = ("d",) + p.dma_slot
                            val = p.dma_val
                        else:
                            if p.sig_val is None or not prog._need_sync(p, op):
                                continue
                            key = ("c", p.eng)
                            val = p.sig_val
                        if val > need.get(key, 0):
                            need[key] = val
                    if op.is_dma and op.dma_val > 16:
                        key = ("d",) + op.dma_slot
                        if op.dma_val - 16 > need.get(key, 0):
                            need[key] = op.dma_val - 16
                    for key, val in need.items():
                        if waited.get(key, 0) >= val:
                            continue
                        e.wait_ge(sems[key], val)
                        waited[key] = val
                    ins = op.fn(e)
                    if op.is_dma:
                        ins.then_inc(sems[("d",) + op.dma_slot], 16)
                    elif op.signal:
                        ins.then_inc(sems[("c", op.eng)], 1)
                if engname in ("sp", "pool"):
                    for slot, val in final_dma.items():
                        if slot[0] == engname:
                            e.wait_ge(sems[("d",) + slot], val)
            return body

        block.tensor(body_for("pe"))
        block.scalar(body_for("act"))
        block.vector(body_for("dve"))
        block.gpsimd(body_for("pool"))
        block.sync(body_for("sp"))

    def close(self):
        self.stack.close()


def _const_tables():
    c = {}
    c["ident"] = np.eye(128, dtype=np.float32)
    perm = np.zeros((128, 128), np.float32)
    for m in range(128):
        d = m % 64
        partner = m + 16 if (d % 32) < 16 else m - 16
        perm[partner, m] = 1.0
    c["permR"] = perm
    t = np.arange(DEC_SEQ)
    row = (t // 64).astype(np.float32)
    col = (t % 64).astype(np.float32)
    freqs = (10000.0 ** (-np.arange(16, dtype=np.float32) / 16)).astype(np.float32)
    cosT = np.zeros((128, DEC_SEQ), np.float32)
    sinT = np.zeros((128, DEC_SEQ), np.float32)
    for p in range(128):
        d = p % 64
        pos = row if d < 32 else col
        ang = (pos * freqs[d % 16]).astype(np.float32)
        cosT[p] = np.cos(ang)
        sinT[p] = np.sin(ang) * (-1.0 if (d % 32) < 16 else 1.0)
    c["ropeC"] = cosT
    c["ropeS"] = sinT
    mk = np.full((64, 64), NEG, np.float32)
    for cq in range(64):
        ws = min(max(cq - 8, 0), 48)
        mk[cq, ws:ws + 16] = 0.0
    c["maskC"] = mk
    bi = np.zeros((2, 8, 4, 65), np.float32)
    sel = np.zeros((2, 8, 4), np.float32)
    bij = np.zeros((2, 8, 2), np.float32)
    hps = np.zeros((8, 128), np.float32)
    for d in range(2):
        for h in range(4):
            bi[d, d * 4 + h, h, :] = 1.0
            sel[d, d * 4 + h, h] = 1.0
            bij[d, d * 4 + h, h // 2] = 1.0
    for r in range(8):
        h = r % 4
        hps[r, (h % 2) * 64:(h % 2) * 64 + 64] = 1.0
    c["BI"] = bi.reshape(2, 8, 260)
    c["SEL"] = sel
    c["BIJ"] = bij
    c["HPSEL"] = hps
    mc = np.zeros((2, 64, 4, 65), np.float32)
    for s in range(64):
        for tt in range(64):
            if s > tt:
                mc[0, s, :, tt] = NEG
            if s < tt:
                mc[1, s, :, tt] = NEG
    c["maskLT"] = mc.reshape(2, 64, 260)
    return c


_CONSTS = None


def run_pipeline(factories, W, stagger, extra=(), extra_every=1, drain=True):
    pending = list(factories)
    free = list(range(W))
    active = []
    rnd = 0
    next_admit = 0
    extra = list(extra)
    while pending or active or (extra and drain):
        if rnd % extra_every == 0 or not (pending or active):
            nx = []
            for g in extra:
                try:
                    next(g)
                    nx.append(g)
                except StopIteration:
                    pass
            extra = nx
        if pending and free and rnd >= next_admit:
            slot = free.pop(0)
            active.append((slot, pending.pop(0)(slot)))
            next_admit = rnd + stagger
        nxt = []
        for slot, g in active:
            try:
                next(g)
                nxt.append((slot, g))
            except StopIteration:
                free.append(slot)
        active = nxt
        rnd += 1


class Builder:
    def __init__(self, depth=DEPTH, debug=False, stop=None):
        self.depth = depth
        self.debug = debug
        self.stop = stop
        self.nc = bass.Bass("TRN2", target_bir_lowering=False)
        self.P = Prog(self.nc)
        self.dbg_outs = []
        self.ring_n = 0

    def din(self, name, shape):
        t = self.nc.dram_tensor(name, list(shape), F32, kind="ExternalInput")
        self.P.reg_dram(t, True)
        return t

    def dout(self, name, shape):
        t = self.nc.dram_tensor(name, list(shape), F32, kind="ExternalOutput")
        self.P.reg_dram(t, False)
        return t

    def dbg(self, name, ap, shape):
        if not self.debug:
            return
        dt = ap.dtype
        t = self.nc.dram_tensor("dbg_" + name, list(shape), dt, kind="ExternalOutput")
        self.P.reg_dram(t, False)
        self.P.dma("sp", t.ap(), ap)
        self.dbg_outs.append("dbg_" + name)

    def chk(self, name):
        if self.stop == name:
            self.P.skip = True

    def carve(self, dtype, shape, nparts=128, at=None):
        n = int(np.prod(shape))
        nbytes = n * _DT_SIZE[dtype]
        if at is None:
            off = (self.ar_off + 31) // 32 * 32
            assert off + nbytes <= self.AR_BYTES, ("arena overflow", off + nbytes)
            self.ar_off = off + nbytes
        else:
            off = at
        self.last_off = off
        base = self.arena if dtype == BF16 else self.arena32
        e0 = off // _DT_SIZE[dtype]
        v = base[0:nparts, e0:e0 + n]
        if len(shape) == 2:
            v = v.rearrange("p (a b) -> p a b", a=shape[0], b=shape[1])
        elif len(shape) == 3:
            v = v.rearrange("p (a b c) -> p a b c", a=shape[0], b=shape[1], c=shape[2])
        return v

    def arena_reset(self):
        self.ar_off = 0

    def ring_load(self, pieces, slot=None):
        if slot is None:
            s = self.ring_n % 2
            self.ring_n += 1
        else:
            s = slot
        slot = self.wr[:, s, :]
        views = []
        for (src, eo, shape) in pieces:
            n = int(np.prod(shape))
            v = self.wr[:, s, eo:eo + n]
            if len(shape) == 2:
                v = v.rearrange("p (a b) -> p a b", a=shape[0], b=shape[1])
            self.P.dma("pool", v, src)
            views.append(v)
        return views

    def wsrc(self, t, layer_off, row0, nk, ncols_total, c0, ncols):
        return bass.AP(t, layer_off + row0 * ncols_total + c0,
                       [[ncols_total, 128], [128 * ncols_total, nk], [1, ncols]])

    def build(self):
        nc, P = self.nc, self.P
        L = self.depth
        xp_d = self.din("xp", [2, SEQ, D_MODEL])
        xs_d = self.din("xs", [DEC_SEQ, D_MODEL])
        cak_d = self.din("cak", [DEPTH, 4, PAST, 128])
        cav_d = self.din("cav", [DEPTH, 4, PAST, 128])
        cbk_d = self.din("cbk", [DEPTH, 4, PAST, 64])
        cbv_d = self.din("cbv", [DEPTH, 4, PAST, 64])
        stc_d = self.din("stc", [DEPTH, 2, 4, 64, 64])
        stn_d = self.din("stn", [DEPTH, 2, 4, 64])
        stm_d = self.din("stm", [DEPTH, 8])
        cvec_d = self.din("cvec", [16, 128])
        w_in_d = self.din("w_in", [DEPTH, D_MODEL, N_IN])
        w_out_d = self.din("w_out", [DEPTH, D_MODEL, D_MODEL])
        ada_w_d = self.din("ada_w", [DEPTH, D_MODEL, 6 * D_MODEL])
        adab_d = self.din("ada_b", [DEPTH * 48, 128])
        lnp_d = self.din("lnp", [128, 128])
        w1_d = self.din("w_mlp1", [DEPTH, D_MODEL, D_FF])
        w2_d = self.din("w_mlp2", [DEPTH, D_FF, D_MODEL])
        gbias_d = self.din("gbias", [DEPTH, 16])
        dlam_d = self.din("dlam", [DEPTH, 256])
        dng_d = self.din("dng", [DEPTH, 128])
        rpb_d = self.din("rpbpad", [DEPTH * 4, 15 * 128])
        mng_d = self.din("mng", [DEPTH, 256])
        c_ident = self.din("ident", [128, 128])
        c_perm = self.din("permR", [128, 128])
        c_ropeC = self.din("ropeC", [128, DEC_SEQ])
        c_ropeS = self.din("ropeS", [128, DEC_SEQ])
        c_maskC = self.din("maskC", [64, 64])
        c_BI = self.din("BI", [2, 8, 260])
        c_SEL = self.din("SEL", [2, 8, 4])
        c_BIJ = self.din("BIJ", [2, 8, 2])
        c_HPSEL = self.din("HPSEL", [8, 128])
        c_maskLT = self.din("maskLT", [2, 64, 260])

        yp_d = self.dout("y_prompt", [2, SEQ, D_MODEL])
        ys_d = self.dout("y_sample", [DEC_SEQ, D_MODEL])
        nak_d = self.dout("new_a_k", [2, DEPTH, 4, SEQ, 128])
        nav_d = self.dout("new_a_v", [2, DEPTH, 4, SEQ, 128])
        nbk_d = self.dout("new_b_k", [2, DEPTH, 4, SEQ, 64])
        nbv_d = self.dout("new_b_v", [2, DEPTH, 4, SEQ, 64])
        ncs_d = self.dout("new_c", [2, DEPTH, 2, 4, 64, 64])
        nns_d = self.dout("new_n", [2, DEPTH, 2, 4, 64])
        nms_d = self.dout("new_m", [2, DEPTH, 8])
        gscr = nc.dram_tensor("gscr", [DEPTH * 4, 64, 1920], F32)
        P.reg_dram(gscr, False)

        ps = P.psum_all()
        psb = P.psb
        self.x = x = P.sbuf("x", [128, 8, NT], F32)
        self.h = h = P.sbuf("h", [128, 8, NT], BF16)
        self.wr = P.sbuf("wr", [128, 2, 8192], BF16)
        self.AR_BYTES = 74 * 1024
        self.arena = P.sbuf("arena", [128, self.AR_BYTES // 2], BF16)
        self.arena32 = self.arena.bitcast(F32)
        ropeC = P.sbuf("ropeC_s", [128, DEC_SEQ], F32)
        ropeS = P.sbuf("ropeS_s", [128, DEC_SEQ], F32)
        ident = P.sbuf("ident_s", [128, 128], F32)
        identb = P.sbuf("identb", [128, 128], BF16)
        permR = P.sbuf("permR_s", [128, 128], F32)
        onesb = P.sbuf("onesb", [128, 128], BF16)
        maskC = P.sbuf("maskC_s", [128, 64], F32)
        BI = P.sbuf("BI_s", [8, 2, 260], F32)
        SEL = P.sbuf("SEL_s", [8, 2, 4], F32)
        BIJ = P.sbuf("BIJ_s", [8, 2, 2], F32)
        HPSEL = P.sbuf("HPSEL_s", [8, 128], F32)
        maskLT = P.sbuf("maskLT_s", [64, 2, 260], F32)
        ones8 = P.sbuf("ones8", [8, 64], F32)
        nBI = P.sbuf("nBI", [8, 2, 4, 65], F32)
        lnp = P.sbuf("lnp_s", [128, 128], F32)
        adab = P.sbuf("adab_s", [128, DEPTH * 48], F32)
        scb = P.sbuf("scb", [128, 8, 2], BF16)
        modvs = [P.sbuf("modv0", [128, 48, 2], F32), P.sbuf("modv1", [128, 48, 2], F32)]
        lam = P.sbuf("lam", [128, 8], F32)
        dl = P.sbuf("dl", [128, 256], F32)
        gAb = P.sbuf("gAb", [128, 128], F32)
        gCb = P.sbuf("gCb", [64, 256], F32)
        gb16 = P.sbuf("gb16", [16, 1], F32)
        st4 = P.sbuf("st4", [128, 1024], F32)
        sm = P.sbuf("sm", [128, 96], F32)
        CN = P.sbuf("CN", [128, 2, 2, 65], F32)
        CNb = P.sbuf("CNb", [128, 2, 2, 65], BF16)
        mprev = P.sbuf("mprev", [8, 20], F32)
        mfin = P.sbuf("mfin", [8, 4], F32)
        LD = P.sbuf("LDs", [8, 16], F32)
        small = P.sbuf("small", [128, 64], F32)

        P.dma("sp", ident[:], c_ident.ap())
        P.dma("sp", permR[:], c_perm.ap())
        P.dma("sp", ropeC[:], c_ropeC.ap())
        P.dma("sp", ropeS[:], c_ropeS.ap())
        P.dma("sp", maskC[0:64, :], c_maskC.ap())
        P.dma("sp", maskC[64:128, :], c_maskC.ap())
        P.dma("sp", BI[:], bass.AP(c_BI, 0, [[260, 8], [8 * 260, 2], [1, 260]]))
        P.dma("sp", SEL[:], bass.AP(c_SEL, 0, [[4, 8], [32, 2], [1, 4]]))
        P.dma("sp", BIJ[:], bass.AP(c_BIJ, 0, [[2, 8], [16, 2], [1, 2]]))
        P.dma("sp", HPSEL[:], c_HPSEL.ap())
        P.dma("sp", maskLT[:], bass.AP(c_maskLT, 0, [[260, 64], [64 * 260, 2], [1, 260]]))
        P.copy(identb[:], ident[:])
        P.ts(nBI[:].rearrange("p d h e -> p (d h e)"), BI[:].rearrange("p d q -> p (d q)"), -1.0, None, ALU.mult)
        P.memset(onesb[:], 1.0)
        P.memset(ones8[:], 1.0)
        P.dma("sp", gscr.ap(), bass.AP(rpb_d, 0, [[1920, DEPTH * 4], [0, 64], [1, 1920]]))
        P.dma("sp", st4[:, 0:128], lnp_d.ap())
        P.transpose(ps[:, 0:128], st4[:, 0:128], ident[:])
        P.copy(lnp[:], ps[:, 0:128])
        for i in range(2):
            r0 = i * 96
            P.dma("sp", st4[0:96, 128:256], adab_d.ap()[r0:r0 + 96, :])
            P.transpose(ps[:, 512:512 + 96], st4[0:96, 128:256], ident[0:96, 0:96])
            P.copy(adab[:, r0:r0 + 96], ps[:, 512:512 + 96])
        P.dma("sp", st4[0:16, 256:384], cvec_d.ap())
        P.transpose(ps[:, 1024:1040], st4[0:16, 256:384], ident[0:16, 0:16])
        P.act(small[:, 0:16], ps[:, 1024:1040], AF.Sigmoid)
        P.tt(scb[:].rearrange("p k c -> p c k"), small[:, 0:16].rearrange("p (c k) -> p c k", c=2, k=8),
             ps[:, 1024:1040].rearrange("p (c k) -> p c k", c=2, k=8), ALU.mult)

        self.arena_reset()
        stgs = [self.carve(F32, (1024,)) for _ in range(6)]
        for blk in range(12):
            if blk < 4:
                src = xp_d.ap()[blk // 2, (blk % 2) * 128:(blk % 2) * 128 + 128, :]
            else:
                src = xs_d.ap()[(blk - 4) * 128:(blk - 4) * 128 + 128, :]
            stg = stgs[blk % 6]
            P.dma("sp", stg, src)
            for half in range(2):
                b = P.bank()
                for c4 in range(4):
                    c = half * 4 + c4
                    P.transpose(ps[:, b * 512 + c4 * 128: b * 512 + c4 * 128 + 128], stg[:, c * 128:(c + 1) * 128], ident[:])
                P.copy(x[:, half * 4:half * 4 + 4, blk * 128:(blk + 1) * 128],
                       ps[:, b * 512:(b + 1) * 512].rearrange("p (c t) -> p c t", c=4, t=128),
                       eng="act" if half else "dve")

        tiles = [(0, 0), (1, 1), (2, 1)]

        for l in range(L):
            lam_init = 0.8 - 0.6 * math.exp(-0.3 * l)
            P.dma("sp", dl[:], bass.AP(dlam_d, l * 256, [[0, 128], [1, 256]]))
            P.tt(dl[:, 0:64], dl[:, 0:64], dl[:, 64:128], ALU.mult)
            P.tt(dl[:, 128:192], dl[:, 128:192], dl[:, 192:256], ALU.mult)
            P.reduce(lam[:, 0:1], dl[:, 0:64], ALU.add)
            P.reduce(lam[:, 1:2], dl[:, 128:192], ALU.add)
            P.act(lam[:, 2:4], lam[:, 0:2], AF.Exp)
            P.tt(lam[:, 4:5], lam[:, 2:3], lam[:, 3:4], ALU.subtract)
            P.ts(lam[:, 5:6], lam[:, 4:5], -1.0, -lam_init, ALU.mult, ALU.add)
            neglam = lam[:, 5:6]
            P.dma("sp", gAb[:], bass.AP(dng_d, l * 128, [[0, 128], [1, 128]]))
            P.ts(gAb[:], gAb[:], 1.0 - lam_init, None, ALU.mult)
            P.dma("sp", gCb[:], bass.AP(mng_d, l * 256, [[0, 64], [1, 256]]))
            P.dma("sp", gb16[:], bass.AP(gbias_d, l * 16, [[1, 16], [1, 1]]))

            self.chk("load")
            modv = modvs[l % 2]

            def ada_gen(la, mv, slot=None):
                for g in range(6):
                    (wv,) = self.ring_load([(self.wsrc(ada_w_d, la * D_MODEL * 6144, 0, 8, 6144, g * 1024, 1024), 0, (8, 1024))], slot=slot)
                    for cc in range(8):
                        j = g * 8 + cc
                        bk, col = (6, 448 + 2 * j) if j < 24 else (7, 448 + 2 * (j - 24))
                        for k in range(8):
                            P.mm(ps[:, bk * 512 + col: bk * 512 + col + 2], wv[:, k, cc * 128:(cc + 1) * 128], scb[:, k, :],
                                 start=(k == 0), stop=(k == 7))
                        yield
                for hf in range(2):
                    bk = 6 + hf
                    P.tt(mv[:, hf * 24:(hf + 1) * 24, :], ps[:, bk * 512 + 448: bk * 512 + 496].rearrange("p (j c) -> p j c", j=24, c=2),
                         adab[:, la * 48 + hf * 24: la * 48 + (hf + 1) * 24].unsqueeze(2).to_broadcast([128, 24, 2]), ALU.add)
                P.ts(mv[:, 8:16, :], mv[:, 8:16, :], 1.0, None, ALU.add)
                P.ts(mv[:, 32:40, :], mv[:, 32:40, :], 1.0, None, ALU.add)
                P.ts(mv[:, 16:24, :], mv[:, 16:24, :], 1.0 / ALPHA, None, ALU.mult)
                P.ts(mv[:, 40:48, :], mv[:, 40:48, :], 1.0 / ALPHA, None, ALU.mult)
                yield

            if l == 0:
                for _ in ada_gen(0, modv):
                    pass

            def modulate(which_sh, which_sc):
                n = 0
                for c in range(8):
                    for (cond, t0, t1) in ((0, 0, 512), (1, 512, NT)):
                        sc_ap = modv[:, which_sc * 8 + c, cond:cond + 1]
                        sh_ap = modv[:, which_sh * 8 + c, cond:cond + 1]
                        if n % 2 == 0:
                            P.ts(h[:, c, t0:t1], x[:, c, t0:t1], sc_ap, sh_ap, ALU.mult, ALU.add)
                        else:
                            P.act(h[:, c, t0:t1], x[:, c, t0:t1], AF.Identity, bias=sh_ap, scale=sc_ap)
                        n += 1

            def wout_partial(wo, nk, mixT, tile_list, gidx):
                for (ti, cond, mcol) in tile_list:
                    for fc in range(8):
                        b = P.bank()
                        for k in range(nk):
                            P.mm(ps[:, b * 512:(b + 1) * 512], wo[:, k, fc * 128:(fc + 1) * 128], mixT[:, k, mcol:mcol + 512],
                                 start=(k == 0), stop=(k == nk - 1))
                        P.stt(x[:, fc, ti * 512:(ti + 1) * 512], ps[:, b * 512:(b + 1) * 512],
                              modv[:, gidx * 8 + fc, cond:cond + 1], x[:, fc, ti * 512:(ti + 1) * 512], ALU.mult, ALU.add)

            def layer_norm(vec_g, vec_b):
                self.arena_reset()
                sqs = [self.carve(BF16, (8, 512)) for _ in range(3)]
                xbs = [self.carve(BF16, (8, 512)) for _ in range(3)]
                means = [self.carve(F32, (512,)) for _ in range(3)]
                rstds = [self.carve(F32, (512,)) for _ in range(3)]
                tmpv = self.carve(F32, (512,))
                for ti in range(3):
                    tsl = slice(ti * 512, (ti + 1) * 512)
                    sq, xb, mean, rstd = sqs[ti], xbs[ti], means[ti], rstds[ti]
                    for c in range(8):
                        if c % 2 == 0:
                            P.copy(xb[:, c, :], x[:, c, tsl])
                        else:
                            P.copy(xb[:, c, :], x[:, c, tsl], eng="act")
                        P.act(sq[:, c, :], x[:, c, tsl], AF.Square)
                    b1 = 2 * ti
                    b2 = 2 * ti + 1
                    for c in range(8):
                        P.mm(ps[:, b1 * 512:(b1 + 1) * 512], onesb[:], xb[:, c, :], start=(c == 0), stop=(c == 7))
                    for c in range(8):
                        P.mm(ps[:, b2 * 512:(b2 + 1) * 512], onesb[:], sq[:, c, :], start=(c == 0), stop=(c == 7))
                for ti in range(3):
                    mean, rstd = means[ti], rstds[ti]
                    b1, b2 = 2 * ti, 2 * ti + 1
                    P.ts(mean, ps[:, b1 * 512:(b1 + 1) * 512], 1.0 / 1024, None, ALU.mult)
                    P.tt(tmpv, mean, mean, ALU.mult)
                    P.stt(rstd, ps[:, b2 * 512:(b2 + 1) * 512], 1.0 / 1024, tmpv, ALU.mult, ALU.subtract)
                    P.ts(rstd, rstd, EPS / (ALPHA * ALPHA), None, ALU.add)
                    P.act(rstd, rstd, AF.Ln)
                    P.act(rstd, rstd, AF.Exp, scale=-0.5)
                for ti in range(3):
                    tsl = slice(ti * 512, (ti + 1) * 512)
                    mean, rstd = means[ti], rstds[ti]
                    for c in range(8):
                        P.tt(x[:, c, tsl], x[:, c, tsl], mean, ALU.subtract)
                        P.tt(x[:, c, tsl], x[:, c, tsl], rstd, ALU.mult)
                        gcol = lnp[:, vec_g * 32 + l * 8 + c: vec_g * 32 + l * 8 + c + 1]
                        bcol = lnp[:, vec_b * 32 + l * 8 + c: vec_b * 32 + l * 8 + c + 1]
                        P.act(x[:, c, tsl], x[:, c, tsl], AF.Identity, bias=bcol, scale=gcol)

            modulate(0, 1)
            if self.debug and l == 0:
                self.dbg("h1", h[:], [128, 8, NT])
            self.chk("mod")

            self.arena_reset()
            qT = self.carve(BF16, (4, NT))
            kT = self.carve(BF16, (4, 2048))
            vA = self.carve(BF16, (16, 512))
            mixT = qT
            eSb = [self.carve(BF16, (2, 1536))]
            es0_off = self.last_off
            eSb.append(self.carve(BF16, (2, 1536)))
            ckTM = self.carve(BF16, (4, 4, 128), at=self.last_off)
            ATb = [self.carve(BF16, (12, 128)), self.carve(BF16, (12, 128))]
            at1 = self.last_off
            ropet = [(self.carve(F32, (512,), at=at1 - 3072), self.carve(F32, (512,), at=at1 - 3072 + 2048)),
                     (self.carve(F32, (512,), at=es0_off), self.carve(F32, (512,), at=es0_off + 2048))]
            rope_n = [0]
            onbb = [self.carve(BF16, (128,)), self.carve(BF16, (128,))]
            junk = self.carve(F32, (128,))
            for ch in range(4):
                P.dma("pool", vA[:, 4 + ch, :].rearrange("p (h e) -> p h e", h=4, e=128),
                      bass.AP(cav_d, l * 4 * PAST * 128 + ch * 128 * 128, [[128, 128], [PAST * 128, 4], [1, 128]]))
                P.dma("pool", ckTM[:, ch, :, :],
                      bass.AP(cak_d, l * 4 * PAST * 128 + ch * 128 * 128, [[128, 128], [PAST * 128, 4], [1, 128]]))
            woff = l * D_MODEL * N_IN
            a1_slot = self.ring_n % 2
            (wA1,) = self.ring_load([(self.wsrc(w_in_d, woff, 0, 8, N_IN, 0, 1024), 0, (8, 1024))])
            wA2, woA = self.ring_load([
                (self.wsrc(w_in_d, woff, 0, 8, N_IN, 1024, 512), 0, (8, 512)),
                (self.wsrc(w_out_d, l * D_MODEL * D_MODEL, 0, 4, D_MODEL, 0, 1024), 4096, (4, 1024))])
            for hd in range(4):
                b = P.bank()
                for ch in range(4):
                    P.transpose(psb[:, b * 1024 + ch * 128: b * 1024 + ch * 128 + 128], ckTM[:, ch, hd, :], identb[:])
                P.copy(kT[:, hd, 512:1024], psb[:, b * 1024: b * 1024 + 512], eng="act")
            for (ti, cond) in tiles:
                for cc in range(8):
                    b = P.bank()
                    for k in range(8):
                        P.mm(ps[:, b * 512:(b + 1) * 512], wA1[:, k, cc * 128:(cc + 1) * 128], h[:, k, ti * 512:(ti + 1) * 512],
                             start=(k == 0), stop=(k == 7))
                    if cc < 4:
                        dst = qT[:, cc, ti * 512:(ti + 1) * 512]
                    else:
                        dst = kT[:, cc - 4, 0:512] if ti == 0 else kT[:, cc - 4, 1024 + (ti - 1) * 512: 1024 + ti * 512]
                    if ti == 0:
                        P.copy(dst, ps[:, b * 512:(b + 1) * 512], eng="act")
                    else:
                        tk = slice((ti - 1) * 512, ti * 512)
                        xs32, rt1 = ropet[rope_n[0] % 2]
                        rope_n[0] += 1
                        P.copy(xs32, ps[:, b * 512:(b + 1) * 512], eng="act")
                        b2 = P.bank()
                        P.mm(ps[:, b2 * 512:(b2 + 1) * 512], permR[:], xs32)
                        P.tt(rt1, xs32, ropeC[:, tk], ALU.mult)
                        P.tt(xs32, ps[:, b2 * 512:(b2 + 1) * 512], ropeS[:, tk], ALU.mult)
                        P.tt(dst, rt1, xs32, ALU.add)
            for blk in range(4):
                b = P.bank()
                for k in range(8):
                    P.mm(ps[:, b * 512:(b + 1) * 512], h[:, k, blk * 128:(blk + 1) * 128], wA1[:, k, 512:1024],
                         start=(k == 0), stop=(k == 7))
                P.copy(st4[:, 0:512], ps[:, b * 512:(b + 1) * 512], eng="act")
                s_, t0 = blk // 2, (blk % 2) * 128
                P.dma("sp", bass.AP(nak_d, ((s_ * DEPTH + l) * 4) * SEQ * 128 + t0 * 128, [[128, 128], [SEQ * 128, 4], [1, 128]]),
                      st4[:, 0:512].rearrange("p (h e) -> p h e", h=4, e=128))
            for blk in range(12):
                b = P.bank()
                for k in range(8):
                    P.mm(ps[:, b * 512:(b + 1) * 512], h[:, k, blk * 128:(blk + 1) * 128], wA2[:, k, :],
                         start=(k == 0), stop=(k == 7))
                vidx = blk if blk < 4 else blk + 4
                P.copy(vA[:, vidx, :], ps[:, b * 512:(b + 1) * 512], eng="act")
                if blk < 4:
                    P.copy(st4[:, 512:1024], ps[:, b * 512:(b + 1) * 512])
                    s_, t0 = blk // 2, (blk % 2) * 128
                    P.dma("sp", bass.AP(nav_d, ((s_ * DEPTH + l) * 4) * SEQ * 128 + t0 * 128, [[128, 128], [SEQ * 128, 4], [1, 128]]),
                          st4[:, 512:1024].rearrange("p (h e) -> p h e", h=4, e=128))

            self.chk("Aproj")

            def attn_A_threads(q0, nqt, k0, nkeys, vblk0, bufs=None, sbank=None, mbank=None):
                scale = 0.125
                nkb = (nkeys + 511) // 512
                nkc = nkeys // 128
                facs = []
                for qt in range(nqt):
                    for hd in range(4):
                        def th(slot, qt=qt, hd=hd):
                            qc = q0 + qt * 128
                            if bufs is None:
                                eS, AT, onb = eSb[slot], ATb[slot], onbb[slot]
                                b = 3 * slot
                            else:
                                eS, AT, onb = bufs[slot]
                                b = sbank(slot)
                            so = 16 * slot
                            st = lambda a, b_: sm[:, so + a:so + b_]
                            for m in range(2):
                                for i in range(nkb):
                                    w = min(512, nkeys - i * 512)
                                    P.mm(ps[:, (b + i) * 512:(b + i) * 512 + w], qT[m * 64:(m + 1) * 64, hd, qc:qc + 128],
                                         kT[m * 64:(m + 1) * 64, hd, k0 + i * 512:k0 + i * 512 + w])
                                yield
                                P.reduce(st(m, m + 1), ps[:, b * 512:b * 512 + nkeys], ALU.max)
                                P.ts(st(2 + m, 3 + m), st(m, m + 1), -scale, None, ALU.mult)
                                P.act(eS[:, m, 0:nkeys], ps[:, b * 512:b * 512 + nkeys], AF.Exp, bias=st(2 + m, 3 + m), scale=scale,
                                      accum_out=st(4 + m, 5 + m))
                                yield
                            P.recip(st(6, 8), st(4, 6))
                            P.tt(st(8, 9), st(7, 8), neglam, ALU.mult)
                            P.ts(eS[:, 0, 0:nkeys], eS[:, 0, 0:nkeys], st(6, 7), None, ALU.mult)
                            P.stt(eS[:, 0, 0:nkeys], eS[:, 1, 0:nkeys], st(8, 9), eS[:, 0, 0:nkeys], ALU.mult, ALU.add)
                            yield
                            bt_ = (6 + slot) if mbank is None else mbank(slot)
                            for g0 in range(0, nkc, 4):
                                gn = min(4, nkc - g0)
                                for kc in range(gn):
                                    P.transpose(psb[:, bt_ * 1024 + kc * 128: bt_ * 1024 + kc * 128 + 128],
                                                eS[:, 0, (g0 + kc) * 128:(g0 + kc + 1) * 128], identb[:])
                                P.copy(AT[:, g0:g0 + gn, :], psb[:, bt_ * 1024: bt_ * 1024 + gn * 128].rearrange("p (a b) -> p a b", a=gn, b=128),
                                       eng="act")
                                yield
                            ops_ = ps[:, bt_ * 512 + 256: bt_ * 512 + 384]
                            for kc in range(nkc):
                                P.mm(ops_, AT[:, kc, :], vA[:, vblk0 + kc, hd * 128:(hd + 1) * 128],
                                     start=(kc == 0), stop=(kc == nkc - 1))
                            yield
                            P.act(junk, ops_, AF.Square, accum_out=st(9, 10))
                            P.ts(st(10, 11), st(9, 10), 1.0 / 128, EPS, ALU.mult, ALU.add)
                            P.act(st(10, 11), st(10, 11), AF.Ln)
                            P.act(st(11, 12), st(10, 11), AF.Exp, scale=-0.5)
                            P.stt(onb, ops_, st(11, 12), gAb[:], ALU.mult, ALU.mult)
                            yield
                            P.transpose(psb[:, bt_ * 1024 + 768: bt_ * 1024 + 896], onb, identb[:])
                            P.copy(mixT[:, hd, qc:qc + 128], psb[:, bt_ * 1024 + 768: bt_ * 1024 + 896], eng="act")
                        facs.append(th)
                return facs

            ada_extra = [ada_gen(l + 1, modvs[(l + 1) % 2], slot=a1_slot)] if l + 1 < L else []
            pbufs = [(self.carve(BF16, (2, 256)), self.carve(BF16, (2, 128)), self.carve(BF16, (128,))) for _ in range(4)]
            run_pipeline(attn_A_threads(0, 2, 0, 256, 0, pbufs, lambda sl: sl, lambda sl: 4 + sl)
                         + attn_A_threads(256, 2, 256, 256, 2, pbufs, lambda sl: sl, lambda sl: 4 + sl), 4, 3,
                         extra=ada_extra, extra_every=6, drain=False)
            run_pipeline(attn_A_threads(512, 8, 512, 1536, 4), 2, 4, extra=ada_extra, extra_every=4)
            if self.debug and l == 0:
                self.dbg("mixA", mixT, [128, 4, NT])
            wout_partial(woA, 4, mixT, [(0, 0, 0), (1, 1, 512), (2, 1, 1024)], 2)

            self.chk("A")
            self.arena_reset()
            qB = self.carve(BF16, (2, NT))
            kB = self.carve(BF16, (2, 2048))
            vB = self.carve(BF16, (16, 256))
            vBs = self.carve(BF16, (8, 256))
            mixB = self.carve(BF16, (2, NT))
            tband = self.carve(BF16, (4, 15, 64))
            tbst = self.carve(F32, (15, 64))
            eBb = [self.carve(BF16, (1024,)) for _ in range(4)]
            eTBb = [self.carve(BF16, (8, 128)) for _ in range(4)]
            obBb = [self.carve(BF16, (256,)) for _ in range(2)]
            cbkTM = self.carve(BF16, (4, 4, 64))
            wB, woB = self.ring_load([
                (self.wsrc(w_in_d, woff, 0, 8, N_IN, 1536, 768), 0, (8, 768)),
                (self.wsrc(w_out_d, l * D_MODEL * D_MODEL, 512, 2, D_MODEL, 0, 1024), 6144, (2, 1024))])
            for ch in range(4):
                P.dma("pool", vB[:, 4 + ch, :].rearrange("p (h e) -> p h e", h=4, e=64),
                      bass.AP(cbv_d, l * 4 * PAST * 64 + ch * 128 * 64, [[64, 128], [PAST * 64, 4], [1, 64]]))
                P.dma("pool", cbkTM[:, ch, :, :],
                      bass.AP(cbk_d, l * 4 * PAST * 64 + ch * 128 * 64, [[64, 128], [PAST * 64, 4], [1, 64]]))
            for j in range(2):
                b = P.bank()
                for ch in range(4):
                    P.transpose(psb[:, b * 1024 + ch * 128: b * 1024 + ch * 128 + 128],
                                cbkTM[:, ch, 2 * j:2 * j + 2, :], identb[:])
                P.copy(kB[:, j, 512:1024], psb[:, b * 1024: b * 1024 + 512], eng="act")
            for hd in range(4):
                for half in range(2):
                    P.dma("sp", tbst[half * 64:(half + 1) * 64], bass.AP(gscr, ((l * 4 + hd) * 64) * 1920 + 63, [[1919, 64], [128, 15], [1, 64]]))
                P.tt(tbst, tbst, maskC[:].unsqueeze(1).to_broadcast([128, 15, 64]), ALU.add)
                P.ts(tband[:, hd, :, :], tbst, 8.0, None, ALU.mult)
            for (ti, cond) in tiles:
                for cc in range(4):
                    b = P.bank()
                    for k in range(8):
                        P.mm(ps[:, b * 512:(b + 1) * 512], wB[:, k, cc * 128:(cc + 1) * 128], h[:, k, ti * 512:(ti + 1) * 512],
                             start=(k == 0), stop=(k == 7))
                    if cc < 2:
                        dst = qB[:, cc, ti * 512:(ti + 1) * 512]
                    else:
                        dst = kB[:, cc - 2, 0:512] if ti == 0 else kB[:, cc - 2, 1024 + (ti - 1) * 512: 1024 + ti * 512]
                    P.copy(dst, ps[:, b * 512:(b + 1) * 512], eng="act")
            for blk in range(12):
                b = P.bank()
                for k in range(8):
                    P.mm(ps[:, b * 512: b * 512 + 512], h[:, k, blk * 128:(blk + 1) * 128], wB[:, k, 256:768],
                         start=(k == 0), stop=(k == 7))
                vidx = blk if blk < 4 else blk + 4
                P.copy(vB[:, vidx, :], ps[:, b * 512 + 256: b * 512 + 512], eng="act")
                if blk < 4:
                    P.copy(st4[:, 0:512], ps[:, b * 512: b * 512 + 512])
                    s_, t0 = blk // 2, (blk % 2) * 128
                    P.dma("sp", bass.AP(nbk_d, ((s_ * DEPTH + l) * 4) * SEQ * 64 + t0 * 64, [[64, 128], [SEQ * 64, 4], [1, 64]]),
                          st4[:, 0:256].rearrange("p (h e) -> p h e", h=4, e=64))
                    P.dma("sp", bass.AP(nbv_d, ((s_ * DEPTH + l) * 4) * SEQ * 64 + t0 * 64, [[64, 128], [SEQ * 64, 4], [1, 64]]),
                          st4[:, 256:512].rearrange("p (h e) -> p h e", h=4, e=64))
            for jb in range(7):
                b = P.bank()
                t0 = 512 + 64 + jb * 128
                for k in range(8):
                    P.mm(ps[:, b * 512: b * 512 + 256], h[:, k, t0:t0 + 128], wB[:, k, 512:768], start=(k == 0), stop=(k == 7))
                P.copy(vBs[:, jb, :], ps[:, b * 512: b * 512 + 256], eng="act")
            scaleB = 0.125

            def bp_threads():
                facs = []
                for s_ in range(2):
                    for qt in range(2):
                        for hd in range(4):
                            def th(slot, s_=s_, qt=qt, hd=hd):
                                qc = s_ * 256 + qt * 128
                                j, hp = hd // 2, hd % 2
                                eB, eTB, obB = eBb[slot], eTBb[slot], obBb[(s_ * 2 + qt) % 2]
                                so = 32 + 10 * slot
                                st = lambda a, b_: sm[:, so + a:so + b_]
                                b = 2 * slot
                                bt = 2 * slot + 1
                                P.mm(ps[:, b * 512: b * 512 + 256], qB[hp * 64:(hp + 1) * 64, j, qc:qc + 128],
                                     kB[hp * 64:(hp + 1) * 64, j, s_ * 256:(s_ + 1) * 256])
                                yield
                                P.reduce(st(0, 1), ps[:, b * 512: b * 512 + 256], ALU.max)
                                P.ts(st(1, 2), st(0, 1), -scaleB, None, ALU.mult)
                                P.act(eB[:, 0:256], ps[:, b * 512: b * 512 + 256], AF.Exp, bias=st(1, 2), scale=scaleB, accum_out=st(2, 3))
                                P.recip(st(3, 4), st(2, 3))
                                yield
                                for kc in range(2):
                                    P.transpose(psb[:, bt * 1024 + kc * 128: bt * 1024 + kc * 128 + 128], eB[:, kc * 128:(kc + 1) * 128], identb[:])
                                P.copy(eTB[:, 0:2, :], psb[:, bt * 1024: bt * 1024 + 256].rearrange("p (a b) -> p a b", a=2, b=128), eng="act")
                                yield
                                ops_ = ps[:, bt * 512 + 256: bt * 512 + 320]
                                for kc in range(2):
                                    P.mm(ops_, eTB[:, kc, :], vB[:, 2 * s_ + kc, hd * 64:(hd + 1) * 64], start=(kc == 0), stop=(kc == 1))
                                yield
                                P.ts(obB[:, hd * 64:(hd + 1) * 64], ops_, st(3, 4), None, ALU.mult)
                                yield
                                if hd == 3:
                                    for jj in range(2):
                                        P.transpose(psb[:, bt * 1024 + 768 + jj * 128: bt * 1024 + 896 + jj * 128], obB[:, jj * 128:(jj + 1) * 128], identb[:])
                                    P.copy(mixB[:, :, qc:qc + 128], psb[:, bt * 1024 + 768: bt * 1024 + 1024].rearrange("p (a b) -> p a b", a=2, b=128), eng="act")
                            facs.append(th)
                return facs

            def bs_threads():
                facs = []
                for r in range(16):
                    for hd in range(4):
                        def th(slot, r=r, hd=hd):
                            ks = min(max(r - 4, 0), 8)
                            dr0 = ks - r + 7
                            qc = 512 + r * 64
                            j, hp = hd // 2, hd % 2
                            eB, eTB, obB = eBb[slot], eTBb[slot], obBb[r % 2]
                            so = 32 + 10 * slot
                            st = lambda a, b_: sm[0:64, so + a:so + b_]
                            b = 2 * slot
                            bt = 2 * slot
                            P.mm(ps[0:64, b * 512: b * 512 + 512], qB[hp * 64:(hp + 1) * 64, j, qc:qc + 64],
                                 kB[hp * 64:(hp + 1) * 64, j, 1024 + ks * 64: 1024 + ks * 64 + 512], start=True, stop=False)
                            P.mm(ps[0:64, b * 512: b * 512 + 512], identb[hp * 64:(hp + 1) * 64, hp * 64:(hp + 1) * 64],
                                 tband[hp * 64:(hp + 1) * 64, hd, dr0:dr0 + 8, :].rearrange("p r c -> p (r c)"), start=False, stop=True)
                            P.mm(ps[0:64, (b + 1) * 512: (b + 1) * 512 + 512], qB[hp * 64:(hp + 1) * 64, j, qc:qc + 64],
                                 kB[hp * 64:(hp + 1) * 64, j, 512:1024])
                            yield
                            P.reduce(st(0, 1), ps[0:64, b * 512: b * 512 + 1024], ALU.max)
                            P.ts(st(3, 4), st(0, 1), -scaleB, None, ALU.mult)
                            yield
                            P.act(eB[0:64, 0:1024], ps[0:64, b * 512: b * 512 + 1024], AF.Exp, bias=st(3, 4), scale=scaleB, accum_out=st(6, 7))
                            P.recip(st(7, 8), st(6, 7))
                            yield
                            for kc in range(8):
                                P.transpose(psb[:, bt * 1024 + kc * 64: bt * 1024 + kc * 64 + 64], eB[0:64, kc * 128:(kc + 1) * 128],
                                            identb[0:64, 0:64])
                            P.copy(eTB[:, :, 0:64], psb[:, bt * 1024: bt * 1024 + 512].rearrange("p (a b) -> p a b", a=8, b=64), eng="act")
                            yield
                            ops_ = ps[0:64, bt * 512 + 256: bt * 512 + 320]
                            for kc in range(8):
                                if kc < 4:
                                    if ks % 2 == 0:
                                        vv = vB[:, 8 + ks // 2 + kc, hd * 64:(hd + 1) * 64]
                                    else:
                                        vv = vBs[:, (ks - 1) // 2 + kc, hd * 64:(hd + 1) * 64]
                                else:
                                    vv = vB[:, kc, hd * 64:(hd + 1) * 64]
                                P.mm(ops_, eTB[:, kc, 0:64], vv, start=(kc == 0), stop=(kc == 7))
                            yield
                            P.ts(obB[0:64, hd * 64:(hd + 1) * 64], ops_, st(7, 8), None, ALU.mult)
                            yield
                            if hd == 3:
                                for jj in range(2):
                                    P.transpose(psb[:, bt * 1024 + 768 + jj * 64: bt * 1024 + 832 + jj * 64], obB[0:64, jj * 128:(jj + 1) * 128],
                                                identb[0:64, 0:64])
                                P.copy(mixB[:, :, qc:qc + 64], psb[:, bt * 1024 + 768: bt * 1024 + 896].rearrange("p (a b) -> p a b", a=2, b=64), eng="act")
                        facs.append(th)
                return facs

            run_pipeline(bp_threads() + bs_threads(), 4, 3)
            if self.debug and l == 0:
                self.dbg("mixB", mixB, [128, 2, NT])
            wout_partial(woB, 2, mixB, [(0, 0, 0), (1, 1, 512), (2, 1, 1024)], 2)

            self.chk("B")
            wC1, wCg = self.ring_load([
                (self.wsrc(w_in_d, woff, 0, 8, N_IN, 2304, 512), 0, (8, 512)),
                (self.wsrc(w_in_d, woff, 0, 8, N_IN, 3328, 16), 4096, (8, 16))])
            wC2, woC = self.ring_load([
                (self.wsrc(w_in_d, woff, 0, 8, N_IN, 2816, 512), 0, (8, 512)),
                (self.wsrc(w_out_d, l * D_MODEL * D_MODEL, 768, 2, D_MODEL, 0, 1024), 4096, (2, 1024))])
            for grp in range(2):
                self.arena_reset()
                if grp == 0:
                    T, NCH, g0, gtiles, resets = 512, 8, 0, [(0, 0)], (0, 4)
                else:
                    T, NCH, g0, gtiles, resets = 1024, 16, 512, [(1, 1), (2, 1)], (0,)
                qC = self.carve(BF16, (2, T))
                kC = self.carve(BF16, (2, T))
                kTM = self.carve(BF16, (NCH, 256), nparts=64)
                V1 = self.carve(BF16, (NCH, 4, 65), nparts=64)
                sigO = self.carve(BF16, (NCH, 256), nparts=64)
                hfw = self.carve(BF16, (NCH, 256), nparts=64)
                mixC = qC
                T1 = self.carve(F32, (T,), nparts=16)
                T2 = self.carve(F32, (T,), nparts=16)
                T3 = self.carve(F32, (T,), nparts=16)
                I8 = self.carve(F32, (T,), nparts=8)
                F8 = self.carve(F32, (T,), nparts=8)
                scr = []
                for _t in range(2):
                    scr.append(dict(
                        NMbd=self.carve(F32, (4, 65), nparts=8), DEC=self.carve(F32, (2,), nparts=8),
                        Dx=self.carve(F32, (4, 65), nparts=64), SD=self.carve(BF16, (4, 64), nparts=64),
                        WV=self.carve(BF16, (4, 65), nparts=64), EN=self.carve(F32, (8,), nparts=64),
                        tmpc=self.carve(F32, (4, 65), nparts=64), hs=self.carve(F32, (4, 64), nparts=64),
                        o3=self.carve(BF16, (256,), nparts=64), cst=self.carve(F32, (16,), nparts=64),
                        dcy=self.carve(F32, (2,)), t1c=self.carve(F32, (2, 65))))
                tmpc = scr[0]["tmpc"]
                sgt = tmpc.rearrange("p h e -> p (h e)")[:, 0:256]
                P.memset(V1[:, :, :, 64:65], 1.0)
                for (ti, cond) in gtiles:
                    lt = ti * 512 - g0
                    for cc in range(4):
                        b = P.bank()
                        for k in range(8):
                            P.mm(ps[:, b * 512:(b + 1) * 512], wC1[:, k, cc * 128:(cc + 1) * 128], h[:, k, ti * 512:(ti + 1) * 512],
                                 start=(k == 0), stop=(k == 7))
                        dst = qC[:, cc, lt:lt + 512] if cc < 2 else kC[:, cc - 2, lt:lt + 512]
                        P.copy(dst, ps[:, b * 512:(b + 1) * 512], eng="act")
                    b = P.bank()
                    for k in range(8):
                        P.mm(ps[0:16, b * 512:(b + 1) * 512], wCg[:, k, :], h[:, k, ti * 512:(ti + 1) * 512], start=(k == 0), stop=(k == 7))
                    P.ts(T1[:, lt:lt + 512], ps[0:16, b * 512:(b + 1) * 512], gb16[:, 0:1], None, ALU.add)
                for c in range(NCH):
                    t0 = g0 + c * 64
                    b = P.bank()
                    for k in range(8):
                        P.mm(ps[0:64, b * 512: b * 512 + 256], h[:, k, t0:t0 + 64], wC1[:, k, 256:512], start=(k == 0), stop=(k == 7))
                    P.copy(kTM[:, c, :], ps[0:64, b * 512: b * 512 + 256], eng="act")
                    b = P.bank()
                    for k in range(8):
                        P.mm(ps[0:64, b * 512:(b + 1) * 512], h[:, k, t0:t0 + 64], wC2[:, k, :], start=(k == 0), stop=(k == 7))
                    P.copy(V1[:, c, :, 0:64], ps[0:64, b * 512: b * 512 + 256].rearrange("p (h e) -> p h e", h=4, e=64))
                    P.act(sgt, ps[0:64, b * 512 + 256:(b + 1) * 512], AF.Exp, scale=-1.0)
                    P.ts(sgt, sgt, 1.0, None, ALU.add)
                    P.recip(sigO[:, c, :], sgt, lowp=True)
                    P.tt(sigO[:, c, :], sigO[:, c, :], gCb[:], ALU.mult, eng="pool")
                self.chk("C%dproj" % grp)
                P.act(T2, T1, AF.Abs)
                P.act(T2, T2, AF.Exp, scale=-1.0)
                P.act(T2, T2, AF.Ln, bias=1.0)
                P.stt(T2, T1, 0.0, T2, ALU.min, ALU.subtract)
                P.copy(T3, T1[:, ::-1])
                P.dma("sp", I8[0:4, :], T1[0:4, :])
                P.dma("sp", I8[4:8, :], T3[8:12, :])
                P.dma("sp", F8[0:4, :], T2[4:8, :])
                P.copy(T3, T2[:, ::-1])
                P.dma("sp", F8[4:8, :], T3[12:16, :])
                GC = T1[0:8, :]
                BC = T2[0:8, :]
                Mr = T3[0:8, :]
                P.scan(GC, ones8[:, 0:1].to_broadcast([8, T]), F8, 0.0, ALU.mult, ALU.add)
                P.copy(BC[:, 0:64], GC[:, 0:64])
                P.tt(BC[:, 64:T].rearrange("p (c u) -> p c u", c=NCH - 1, u=64),
                     GC[:, 64:T].rearrange("p (c u) -> p c u", c=NCH - 1, u=64),
                     GC[:, 63:T - 64:64].unsqueeze(2).to_broadcast([8, NCH - 1, 64]), ALU.subtract)
                P.tt(I8, I8, BC, ALU.subtract)
                if grp == 0:
                    P.memset(mprev[:, 0:NCH + 1], 0.0)
                else:
                    P.dma("sp", mprev[:, 0:1], bass.AP(stm_d, l * 8, [[1, 8], [1, 1]]))
                for c in range(NCH):
                    cs = slice(c * 64, (c + 1) * 64)
                    P.scan(Mr[:, cs], I8[:, cs], I8[:, cs], mprev[:, c:c + 1], ALU.max, ALU.max)
                    last = (c + 1 == NCH) or ((c + 1) in resets)
                    if last:
                        if grp == 0:
                            P.tt(mfin[:, (c // 4):(c // 4) + 1], BC[:, c * 64 + 63:c * 64 + 64], Mr[:, c * 64 + 63:c * 64 + 64], ALU.add)
                    else:
                        P.tt(mprev[:, c + 1:c + 2], BC[:, c * 64 + 63:c * 64 + 64], Mr[:, c * 64 + 63:c * 64 + 64], ALU.add)
                P.tt(F8.rearrange("p (c u) -> p c u", c=NCH, u=64), mprev[:, 0:NCH].unsqueeze(2).to_broadcast([8, NCH, 64]),
                     Mr.rearrange("p (c u) -> p c u", c=NCH, u=64), ALU.subtract)
                P.stt(GC, BC, -1.0, Mr, ALU.mult, ALU.subtract)
                P.tt(LD[:, 0:NCH], mprev[:, 0:NCH], Mr[:, 63:T:64], ALU.subtract)
                Er, Nr = F8, GC
                for X in (I8, Mr, Er, Nr):
                    P.copy(T2[0:8, :], X[:, ::-1])
                    P.dma("sp", X[4:8, :], T2[4:8, :])
                if grp == 0:
                    for s_ in range(2):
                        P.dma("sp", bass.AP(nms_d, (s_ * DEPTH + l) * 8, [[1, 4], [1, 1]]), mfin[0:4, s_:s_ + 1])
                        P.dma("sp", bass.AP(nms_d, (s_ * DEPTH + l) * 8 + 4, [[1, 4], [1, 1]]), mfin[4:8, 1 - s_:2 - s_])
                self.chk("C%dgate" % grp)
                nseq = 2 if grp == 0 else 1
                cps = NCH // nseq
                hst = hfw
                gsig = sigO

                def chain(slot, d, s_):
                    S_ = scr[slot]
                    NMbd, DEC, Dx, SD, WV, EN = S_["NMbd"], S_["DEC"], S_["Dx"], S_["SD"], S_["WV"], S_["EN"]
                    tmpc_, hs, o3, cst, dcy, t1c = S_["tmpc"], S_["hs"], S_["o3"], S_["cst"], S_["dcy"], S_["t1c"]
                    b0, b1, b2, b3 = 4 * slot, 4 * slot + 1, 4 * slot + 2, 4 * slot + 3
                    bpar = (b1, b2)
                    if grp == 0:
                        P.memset(CN[:, d, :, :], 0.0)
                    else:
                        P.dma("sp", CN[:, d, :, 0:64], bass.AP(stc_d, (l * 2 + d) * 16384, [[64, 128], [8192, 2], [1, 64]]))
                        P.dma("sp", CN[:, d, :, 64:65], bass.AP(stn_d, (l * 2 + d) * 256, [[1, 128], [128, 2], [1, 1]]))
                    P.copy(CNb[:, d, :, :], CN[:, d, :, :])
                    yield
                    order = range(cps) if d == 0 else range(cps - 1, -1, -1)
                    for ci, cl in enumerate(order):
                        c = s_ * cps + cl
                        cf = c if d == 0 else NCH - 1 - c
                        cs = slice(c * 64, (c + 1) * 64)
                        tok = cs
                        endcol = c * 64 + 63 if d == 0 else c * 64
                        first_half = ci < cps // 2
                        lastc = (ci == cps - 1)
                        do_update = not (lastc and grp == 1)
                        BId = BI[:, d, :].rearrange("p (h e) -> p h e", h=4, e=65)
                        P.tt(NMbd[:, :, 0:64], Mr[:, cs].unsqueeze(1).to_broadcast([8, 4, 64]), nBI[:, d, :, 0:64], ALU.mult)
                        P.tt(NMbd[:, :, 64:65], Mr[:, endcol:endcol + 1].unsqueeze(1).to_broadcast([8, 4, 1]), nBI[:, d, :, 64:65], ALU.mult)
                        zps = ps[0:64, b0 * 512: b0 * 512 + 260]
                        P.mm(zps, I8[:, cs], BI[:, d, :], start=True, stop=False)
                        P.mm(zps, ones8[:, :], NMbd.rearrange("p h e -> p (h e)"), start=False, stop=False)
                        P.mm(zps, ident[0:64, 0:64], maskLT[:, d, :], start=False, stop=True)
                        for hp in range(2):
                            for j in range(2):
                                hd = 2 * j + hp
                                P.mm(ps[0:64, bpar[hp] * 512 + hd * 64: bpar[hp] * 512 + hd * 64 + 64], kC[hp * 64:(hp + 1) * 64, j, tok],
                                     qC[hp * 64:(hp + 1) * 64, j, tok])
                        P.mm(ps[0:64, b3 * 512: b3 * 512 + 4], Er[:, cs], SEL[:, d, :])
                        P.mm(ps[0:64, b3 * 512 + 4: b3 * 512 + 8], Nr[:, cs], SEL[:, d, :])
                        yield
                        P.act(Dx.rearrange("p h e -> p (h e)"), zps, AF.Exp)
                        P.act(EN, ps[0:64, b3 * 512: b3 * 512 + 8], AF.Exp)
                        if do_update:
                            P.ts(DEC, BIJ[:, d, :], LD[:, cf:cf + 1], None, ALU.mult)
                            P.mm(ps[:, b0 * 512 + 384: b0 * 512 + 386], HPSEL[:, :], DEC)
                        yield
                        for hp in range(2):
                            P.stt(SD[:, hp::2, :], ps[0:64, bpar[hp] * 512: bpar[hp] * 512 + 256].rearrange("p (h e) -> p h e", h=4, e=64)[:, hp::2, :],
                                  0.125, Dx[:, hp::2, 0:64], ALU.mult, ALU.mult)
                        for hd in range(4):
                            P.act(WV[:, hd, :], V1[:, c, hd, :], AF.Copy, scale=Dx[:, hd, 64:65])
                        if do_update:
                            P.act(dcy, ps[:, b0 * 512 + 384: b0 * 512 + 386], AF.Exp)
                        yield
                        for hp in range(2):
                            for j in range(2):
                                hd = 2 * j + hp
                                P.mm(ps[0:64, bpar[hp] * 512 + hd * 65: bpar[hp] * 512 + hd * 65 + 65], qC[hp * 64:(hp + 1) * 64, j, tok],
                                     CNb[hp * 64:(hp + 1) * 64, d, j, :])
                        for hd in range(4):
                            P.mm(ps[0:64, b3 * 512 + hd * 65: b3 * 512 + hd * 65 + 65], SD[:, hd, :], V1[:, c, hd, :])
                        yield
                        for hp in range(2):
                            P.stt(tmpc_[:, hp::2, :], ps[0:64, bpar[hp] * 512: bpar[hp] * 512 + 260].rearrange("p (h e) -> p h e", h=4, e=65)[:, hp::2, :],
                                  0.125, EN[:, hp:4:2].unsqueeze(2).to_broadcast([64, 2, 65]), ALU.mult, ALU.mult)
                        P.tt(tmpc_, tmpc_, ps[0:64, b3 * 512: b3 * 512 + 260].rearrange("p (h e) -> p h e", h=4, e=65), ALU.add)
                        P.act(cst[:, 0:4], tmpc_[:, :, 64:65].rearrange("p h e -> p (h e)"), AF.Abs)
                        yield
                        P.tt(cst[:, 0:4], cst[:, 0:4], EN[:, 4:8], ALU.max)
                        P.recip(cst[:, 4:8], cst[:, 0:4])
                        if do_update:
                            for j in range(2):
                                P.mm(ps[:, b3 * 512 + j * 130: b3 * 512 + j * 130 + 130], kTM[:, c, j * 128:(j + 1) * 128],
                                     WV[:, 2 * j:2 * j + 2, :].rearrange("p h e -> p (h e)"))
                            P.tt(t1c, CN[:, d, :, :], dcy.unsqueeze(2).to_broadcast([128, 2, 65]), ALU.mult)
                        yield
                        if first_half:
                            for hd in range(4):
                                P.act(hst[:, c, hd * 64:(hd + 1) * 64], tmpc_[:, hd, 0:64], AF.Copy, scale=cst[:, 4 + hd:5 + hd])
                        else:
                            P.tt(hs, tmpc_[:, :, 0:64], cst[:, 4:8].unsqueeze(2).to_broadcast([64, 4, 64]), ALU.mult)
                            P.tt(hs, hs, hst[:, c, :].rearrange("p (h e) -> p h e", h=4, e=64), ALU.add)
                            hs2 = Dx[:, :, 0:64]
                            for hd in range(4):
                                P.act(hs2[:, hd, :], hs[:, hd, :], AF.Square, accum_out=cst[:, 8 + hd:9 + hd])
                            P.ts(cst[:, 8:12], cst[:, 8:12], 1.0 / 64, EPS, ALU.mult, ALU.add)
                            P.act(cst[:, 8:12], cst[:, 8:12], AF.Ln)
                            P.act(cst[:, 12:16], cst[:, 8:12], AF.Exp, scale=-0.5)
                        if do_update:
                            for hp in range(2):
                                pv = ps[hp * 64:(hp + 1) * 64, b3 * 512: b3 * 512 + 260].rearrange("p (j q) -> p j q", j=2, q=130)
                                P.tt(CN[hp * 64:(hp + 1) * 64, d, :, :], t1c[hp * 64:(hp + 1) * 64, :, :],
                                     pv[:, :, hp * 65:hp * 65 + 65], ALU.add)
                            P.copy(CNb[:, d, :, :], CN[:, d, :, :], eng="act")
                        yield
                        if not first_half:
                            P.tt(hs, hs, cst[:, 12:16].unsqueeze(2).to_broadcast([64, 4, 64]), ALU.mult)
                            P.tt(o3.rearrange("p (h e) -> p h e", h=4, e=64), hs, gsig[:, c, :].rearrange("p (h e) -> p h e", h=4, e=64), ALU.mult)
                            for j in range(2):
                                P.transpose(psb[:, b0 * 1024 + 832 + j * 64: b0 * 1024 + 896 + j * 64], o3[:, j * 128:(j + 1) * 128], identb[0:64, 0:64])
                            P.copy(mixC[:, :, tok], psb[:, b0 * 1024 + 832: b0 * 1024 + 960].rearrange("p (a b) -> p a b", a=2, b=64), eng="act")
                        yield
                    if grp == 0:
                        P.dma("sp", bass.AP(ncs_d, ((s_ * DEPTH + l) * 2 + d) * 16384, [[64, 128], [8192, 2], [1, 64]]), CN[:, d, :, 0:64])
                        P.dma("sp", bass.AP(nns_d, ((s_ * DEPTH + l) * 2 + d) * 256, [[1, 128], [128, 2], [1, 1]]), CN[:, d, :, 64:65])

                for s_ in range(nseq):
                    run_pipeline([lambda slot, s_=s_: chain(slot, 0, s_), lambda slot, s_=s_: chain(slot, 1, s_)], 2, 4)
                if self.debug and l == 0:
                    self.dbg("mixC%d" % grp, mixC, [128, 2, T])
                wout_partial(woC, 2, mixC, [(ti, cond, ti * 512 - g0) for (ti, cond) in gtiles], 2)
                if grp == 0:
                    self.chk("C0")

            self.chk("C")
            layer_norm(0, 1)
            if self.debug and l == 0:
                self.dbg("x1", x[:], [128, 8, NT])
            self.chk("ln1")
            modulate(3, 4)
            for e8 in range(8):
                w1v, w2v = self.ring_load([
                    (self.wsrc(w1_d, l * D_MODEL * D_FF, 0, 8, D_FF, e8 * 512, 512), 0, (8, 512)),
                    (self.wsrc(w2_d, l * D_FF * D_MODEL, e8 * 512, 4, D_MODEL, 0, 1024), 4096, (4, 1024))])
                self.arena_reset()
                ub = [self.carve(BF16, (4, 512)) for _ in range(2)]
                rl = [self.carve(BF16, (512,)) for _ in range(2)]
                for (ti, cond) in tiles:
                    u = ub[ti % 2]
                    for fc in range(4):
                        b = P.bank()
                        for k in range(8):
                            P.mm(ps[:, b * 512:(b + 1) * 512], w1v[:, k, fc * 128:(fc + 1) * 128], h[:, k, ti * 512:(ti + 1) * 512],
                                 start=(k == 0), stop=(k == 7))
                        r_ = rl[fc % 2]
                        P.act(r_, ps[:, b * 512:(b + 1) * 512], AF.Relu)
                        P.tt(u[:, fc, :], r_, r_, ALU.mult)
                    for fc in range(8):
                        b = P.bank()
                        for k in range(4):
                            P.mm(ps[:, b * 512:(b + 1) * 512], w2v[:, k, fc * 128:(fc + 1) * 128], u[:, k, :], start=(k == 0), stop=(k == 3))
                        P.stt(x[:, fc, ti * 512:(ti + 1) * 512], ps[:, b * 512:(b + 1) * 512],
                              modv[:, 5 * 8 + fc, cond:cond + 1], x[:, fc, ti * 512:(ti + 1) * 512], ALU.mult, ALU.add)
            layer_norm(2, 3)

        P.skip = False
        self.arena_reset()
        stgs = [self.carve(F32, (1024,)) for _ in range(6)]
        for blk in range(12):
            stg = stgs[blk % 6]
            for half in range(2):
                b = P.bank()
                for c4 in range(4):
                    c = half * 4 + c4
                    P.transpose(ps[:, b * 512 + c4 * 128: b * 512 + c4 * 128 + 128], x[:, c, blk * 128:(blk + 1) * 128], ident[:])
                P.copy(stg[:, half * 512:(half + 1) * 512], ps[:, b * 512:(b + 1) * 512], eng="act" if half else "dve")
            if blk < 4:
                dst = yp_d.ap()[blk // 2, (blk % 2) * 128:(blk % 2) * 128 + 128, :]
            else:
                dst = ys_d.ap()[(blk - 4) * 128:(blk - 4) * 128 + 128, :]
            P.dma("sp", dst, stg)
        P.emit()
        P.close()
        return nc


_PROG_CACHE = {}


def _get_prog(depth=DEPTH, debug=False, stop=None):
    key = (depth, debug, stop)
    if key not in _PROG_CACHE:
        b = Builder(depth, debug, stop)
        nc = b.build()
        _PROG_CACHE[key] = (nc, b.dbg_outs)
    return _PROG_CACHE[key]


def _in_maps(inp):
    global _CONSTS
    if _CONSTS is None:
        _CONSTS = _const_tables()
    f = lambda a: np.ascontiguousarray(np.asarray(a, dtype=np.float32))
    lnp = np.concatenate([f(inp[k]).reshape(32, 128) for k in ("ln1_g", "ln1_b", "ln2_g", "ln2_b")], 0)
    rpb = f(inp["nat_rpb"])
    rpbpad = np.zeros((DEPTH, 4, 15, 128), np.float32)
    rpbpad[..., 48:79] = rpb
    shared = {
        "w_in": f(inp["w_in"]), "w_out": f(inp["w_out"]), "ada_w": f(inp["ada_w"]),
        "ada_b": f(inp["ada_b"]).reshape(DEPTH * 48, 128), "lnp": lnp,
        "w_mlp1": f(inp["w_mlp1"]), "w_mlp2": f(inp["w_mlp2"]),
        "gbias": f(inp["mlstm_gate_bias"]), "dlam": f(inp["diff_lambda"]).reshape(DEPTH, 256),
        "dng": f(inp["diff_norm_g"]), "rpbpad": rpbpad.reshape(DEPTH * 4, 15 * 128),
        "mng": f(inp["mlstm_norm_g"]).reshape(DEPTH, 256),
    }
    shared.update({k: v for k, v in _CONSTS.items()})
    xp = f(inp["x_prompt"]); xs = f(inp["x_sample"])
    maps = []
    for c in range(8):
        b = c // 4
        m = dict(shared)
        m["xp"] = xp[2 * c:2 * c + 2]
        m["xs"] = xs[b]
        m["cak"] = f(inp["cache_a_k"])[b]
        m["cav"] = f(inp["cache_a_v"])[b]
        m["cbk"] = f(inp["cache_b_k"])[b]
        m["cbv"] = f(inp["cache_b_v"])[b]
        m["stc"] = f(inp["state_c"])[b]
        m["stn"] = f(inp["state_n"])[b]
        m["stm"] = f(inp["state_m"])[b].reshape(DEPTH, 8)
        m["cvec"] = np.concatenate([f(inp["c_ctx"]).reshape(8, 128), f(inp["c"])[b].reshape(8, 128)], 0)
        maps.append(m)
    return maps


def kernel(**inputs):
    nc, _ = _get_prog()
    maps = _in_maps(inputs)
    res = run_bass_kernel_spmd(nc, maps, core_ids=list(range(8)))
    r = res.results
    y_prompt = np.concatenate([r[c]["y_prompt"] for c in range(8)], 0)
    y_sample = np.stack([r[0]["y_sample"], r[4]["y_sample"]], 0)
    cat = lambda k: np.concatenate([r[c][k] for c in range(8)], 0)
    new_m = cat("new_m").reshape(BATCH, DEPTH, 2, 4)
    return (y_prompt, y_sample, cat("new_a_k"), cat("new_a_v"), cat("new_b_k"), cat("new_b_v"),
            cat("new_c"), cat("new_n"), new_m)
```

```python
import contextlib
import math
import numpy as np
import concourse.bass as bass
import concourse.mybir as mybir
from concourse.bass_utils import run_bass_kernel_spmd

F32 = mybir.dt.float32
BF16 = mybir.dt.bfloat16
AF = mybir.ActivationFunctionType
ALU = mybir.AluOpType
AX = mybir.AxisListType
_DT_SIZE = {F32: 4, BF16: 2}

D_MODEL = 1024
BATCH = 16
SEQ = 256
DEPTH = 4
DEC_SEQ = 1024
PAST = 512
N_IN = 3344
D_FF = 4096
EPS = 1e-5
ALPHA = (2 * DEPTH) ** 0.25
NEG = -30000.0
NT = 1536


class _Op:
    __slots__ = ("eng", "fn", "deps", "signal", "idx_in_eng", "is_dma", "dma_slot", "dma_val", "sig_val")

    def __init__(self, eng, fn, is_dma):
        self.eng = eng
        self.fn = fn
        self.deps = set()
        self.signal = False
        self.is_dma = is_dma
        self.dma_slot = None
        self.dma_val = None
        self.sig_val = None
        self.idx_in_eng = 0


class Prog:
    ENGS = ("pe", "act", "dve", "pool", "sp")
    NDMA = 8

    def __init__(self, nc):
        self.nc = nc
        self.ops = []
        self.eng_ops = {e: [] for e in self.ENGS}
        self.hist = {}
        self.tinfo = {}
        self.stack = contextlib.ExitStack()
        self.ro = set()
        self.bank_rr = 0
        self.skip = False

    def sbuf(self, name, shape, dtype):
        t = self.stack.enter_context(self.nc.sbuf_tensor(name, list(shape), dtype))
        self.tinfo[t.name] = ("SB", int(np.prod(shape[1:])), _DT_SIZE[dtype])
        return t

    def psum_all(self):
        t = self.stack.enter_context(self.nc.psum_tensor("psall", [128, 4096], F32))
        self.tinfo[t.name] = ("PSUM", 4096, 4)
        self.ps = t
        self.psb = t.bitcast(BF16)
        return t

    def reg_dram(self, t, readonly):
        self.tinfo[t.name] = ("DRAM", 1, 1)
        if readonly:
            self.ro.add(t.name)

    def bank(self, n=1):
        if self.bank_rr + n > 8:
            self.bank_rr = 0
        b = self.bank_rr
        self.bank_rr = (self.bank_rr + n) % 8
        return b

    def _boxes(self, ap):
        name = ap.tensor.name
        info = self.tinfo[name]
        space, rowsize, base_dts = info
        if space == "DRAM":
            if name in self.ro:
                return []
            lo = hi = int(ap.offset)
            for st, cnt in ap.ap:
                ext = st * (cnt - 1)
                if ext < 0:
                    lo += ext
                else:
                    hi += ext
            return [(("D", name), (0, 1, lo, hi + 1))]
        dts = _DT_SIZE[ap.dtype]
        off = int(ap.offset)
        pat = list(ap.ap)
        pstep, pcnt = pat[0]
        rs = rowsize * base_dts // dts
        p0 = off // rs
        st0 = off % rs
        lo = hi = st0
        for st, cnt in pat[1:]:
            ext = st * (cnt - 1)
            if ext < 0:
                lo += ext
            else:
                hi += ext
        b0, b1 = lo * dts, (hi + 1) * dts
        if space == "PSUM":
            return [(("P", bk), (0, 128, 0, 2048)) for bk in range(b0 // 2048, (b1 - 1) // 2048 + 1)]
        return [(("S", name), (p0, p0 + pcnt, b0, b1))]

    @staticmethod
    def _ovl(a, b):
        return a[0] < b[1] and b[0] < a[1] and a[2] < b[3] and b[2] < a[3]

    @staticmethod
    def _covers(a, b):
        return a[0] <= b[0] and a[1] >= b[1] and a[2] <= b[2] and a[3] >= b[3]

    def add(self, eng, fn, reads=(), writes=(), is_dma=False):
        if self.skip:
            return -1
        op = _Op(eng, fn, is_dma)
        idx = len(self.ops)
        self.ops.append(op)
        op.idx_in_eng = len(self.eng_ops[eng])
        self.eng_ops[eng].append(idx)
        accs = []
        for ap in reads:
            if ap is None:
                continue
            for key, box in self._boxes(ap):
                accs.append((key, box, key[0] == "P"))
        for ap in writes:
            if ap is None:
                continue
            for key, box in self._boxes(ap):
                accs.append((key, box, True))
        for key, box, isw in accs:
            h = self.hist.setdefault(key, [])
            for (obox, oidx, oisw) in h:
                if oidx != idx and (isw or oisw) and self._ovl(box, obox):
                    op.deps.add(oidx)
        for key, box, isw in accs:
            h = self.hist[key]
            if isw:
                h[:] = [r for r in h if not (self._covers(box, r[0]) and r[1] != idx)]
            h.append((box, idx, isw))
        return idx

    @staticmethod
    def _isap(v):
        return isinstance(v, bass.AP)

    def mm(self, out, lhsT, rhs, start=True, stop=True):
        return self.add("pe", lambda e: e.matmul(out, lhsT, rhs, start=start, stop=stop), [lhsT, rhs], [out])

    def transpose(self, out, in_, ident):
        return self.add("pe", lambda e: e.transpose(out, in_, ident), [in_, ident], [out])

    def act(self, out, in_, func, bias=None, scale=None, accum_out=None):
        kw = {}
        reads = [in_]
        if bias is not None:
            kw["bias"] = bias
            if self._isap(bias):
                reads.append(bias)
        if scale is not None:
            kw["scale"] = scale
            if self._isap(scale):
                reads.append(scale)
        if accum_out is not None:
            kw["accum_out"] = accum_out
        return self.add("act", lambda e: e.activation(out, in_, func, **kw), reads, [out, accum_out])

    def tt(self, out, in0, in1, op, eng="dve"):
        return self.add(eng, lambda e: e.tensor_tensor(out, in0, in1, op), [in0, in1], [out])

    def ts(self, out, in0, s1, s2=None, op0=ALU.mult, op1=None, eng="dve"):
        reads = [in0] + [s for s in (s1, s2) if self._isap(s)]
        if op1 is None:
            return self.add(eng, lambda e: e.tensor_scalar(out, in0, s1, None, op0), reads, [out])
        return self.add(eng, lambda e: e.tensor_scalar(out, in0, s1, s2, op0, op1), reads, [out])

    def stt(self, out, in0, scalar, in1, op0, op1):
        reads = [in0, in1] + ([scalar] if self._isap(scalar) else [])
        return self.add("dve", lambda e: e.scalar_tensor_tensor(out, in0, scalar, in1, op0, op1), reads, [out])

    def copy(self, out, in_, eng="dve"):
        if eng == "act":
            return self.add("act", lambda e: e.copy(out, in_), [in_], [out])
        return self.add(eng, lambda e: e.tensor_copy(out, in_), [in_], [out])

    def reduce(self, out, in_, op, axis=None):
        ax = axis if axis is not None else AX.X
        return self.add("dve", lambda e: e.tensor_reduce(out, in_, ax, op), [in_], [out])

    def memset(self, ap, val, eng="dve"):
        return self.add(eng, lambda e: e.memset(ap, val), [], [ap])

    def recip(self, out, in_, lowp=False):
        if lowp:
            def fn(e):
                with self.nc.allow_low_precision("bf16 gate output"):
                    return e.reciprocal(out, in_)
            return self.add("dve", fn, [in_], [out])
        return self.add("dve", lambda e: e.reciprocal(out, in_), [in_], [out])

    def scan(self, out, d0, d1, init, op0, op1):
        reads = [d0, d1] + ([init] if self._isap(init) else [])
        return self.add("dve", lambda e: e.tensor_tensor_scan(out, d0, d1, init, op0, op1), reads, [out])

    def dma(self, q, out, in_):
        return self.add(q, lambda e: e.dma_start(out, in_, allow_slow_non_contiguous=True), [in_], [out], is_dma=True)

    def _need_sync(self, p, op):
        if p.eng == op.eng and not op.is_dma and p.eng == "pe":
            return False
        return True

    def emit(self):
        nc = self.nc
        ops = self.ops
        dcount = {"sp": 0, "pool": 0}
        for op in ops:
            if op.is_dma:
                n = dcount[op.eng]
                op.dma_slot = (op.eng, n % self.NDMA)
                op.dma_val = 16 * (n // self.NDMA + 1)
                dcount[op.eng] = n + 1
        for op in ops:
            for d in op.deps:
                p = ops[d]
                if p.is_dma:
                    continue
                if self._need_sync(p, op):
                    p.signal = True
        cnt = {e: 0 for e in self.ENGS}
        for op in ops:
            if op.signal:
                cnt[op.eng] += 1
                op.sig_val = cnt[op.eng]
        sems = {}
        for e in ("pe", "act", "dve", "pool"):
            sems[("c", e)] = self.stack.enter_context(nc.semaphore("s_" + e))
        for q in ("sp", "pool"):
            for k in range(self.NDMA):
                sems[("d", q, k)] = self.stack.enter_context(nc.semaphore("d_%s_%d" % (q, k)))
        final_dma = {}
        for op in ops:
            if op.is_dma:
                final_dma[op.dma_slot] = op.dma_val
        block = self.stack.enter_context(nc.Block())
        prog = self

        def body_for(engname):
            def body(e):
                waited = {}
                for oi in prog.eng_ops[engname]:
                    op = ops[oi]
                    need = {}
                    for d in op.deps:
                        p = ops[d]
                        if p.is_dma:
                            key = ("d",) + p.dma_slot
                            val = p.dma_val
                        else:
                            if p.sig_val is None or not prog._need_sync(p, op):
                                continue
                            key = ("c", p.eng)
                            val = p.sig_val
                        if val > need.get(key, 0):
                            need[key] = val
                    if op.is_dma and op.dma_val > 16:
                        key = ("d",) + op.dma_slot
                        if op.dma_val - 16 > need.get(key, 0):
                            need[key] = op.dma_val - 16
                    for key, val in need.items():
                        if waited.get(key, 0) >= val:
                            continue
                        e.wait_ge(sems[key], val)
                        waited[key] = val
                    ins = op.fn(e)
                    if op.is_dma:
                        ins.then_inc(sems[("d",) + op.dma_slot], 16)
                    elif op.signal:
                        ins.then_inc(sems[("c", op.eng)], 1)
                if engname in ("sp", "pool"):
                    for slot, val in final_dma.items():
                        if slot[0] == engname:
                            e.wait_ge(sems[("d",) + slot], val)
            return body

        block.tensor(body_for("pe"))
        block.scalar(body_for("act"))
        block.vector(body_for("dve"))
        block.gpsimd(body_for("pool"))
        block.sync(body_for("sp"))

    def close(self):
        self.stack.close()


def _const_tables():
    c = {}
    c["ident"] = np.eye(128, dtype=np.float32)
    perm = np.zeros((128, 128), np.float32)
    for m in range(128):
        d = m % 64
        partner = m + 16 if (d % 32) < 16 else m - 16
        perm[partner, m] = 1.0
    c["permR"] = perm
    t = np.arange(DEC_SEQ)
    row = (t // 64).astype(np.float32)
    col = (t % 64).astype(np.float32)
    freqs = (10000.0 ** (-np.arange(16, dtype=np.float32) / 16)).astype(np.float32)
    cosT = np.zeros((128, DEC_SEQ), np.float32)
    sinT = np.zeros((128, DEC_SEQ), np.float32)
    for p in range(128):
        d = p % 64
        pos = row if d < 32 else col
        ang = (pos * freqs[d % 16]).astype(np.float32)
        cosT[p] = np.cos(ang)
        sinT[p] = np.sin(ang) * (-1.0 if (d % 32) < 16 else 1.0)
    c["ropeC"] = cosT
    c["ropeS"] = sinT
    mk = np.full((64, 64), NEG, np.float32)
    for cq in range(64):
        ws = min(max(cq - 8, 0), 48)
        mk[cq, ws:ws + 16] = 0.0
    c["maskC"] = mk
    bi = np.zeros((2, 8, 4, 65), np.float32)
    sel = np.zeros((2, 8, 4), np.float32)
    bij = np.zeros((2, 8, 2), np.float32)
    hps = np.zeros((8, 128), np.float32)
    for d in range(2):
        for h in range(4):
            bi[d, d * 4 + h, h, :] = 1.0
            sel[d, d * 4 + h, h] = 1.0
            bij[d, d * 4 + h, h // 2] = 1.0
    for r in range(8):
        h = r % 4
        hps[r, (h % 2) * 64:(h % 2) * 64 + 64] = 1.0
    c["BI"] = bi.reshape(2, 8, 260)
    c["SEL"] = sel
    c["BIJ"] = bij
    c["HPSEL"] = hps
    mc = np.zeros((2, 64, 4, 65), np.float32)
    for s in range(64):
        for tt in range(64):
            if s > tt:
                mc[0, s, :, tt] = NEG
            if s < tt:
                mc[1, s, :, tt] = NEG
    c["maskLT"] = mc.reshape(2, 64, 260)
    return c


_CONSTS = None


def run_pipeline(factories, W, stagger, extra=(), extra_every=1, drain=True):
    pending = list(factories)
    free = list(range(W))
    active = []
    rnd = 0
    next_admit = 0
    extra = list(extra)
    while pending or active or (extra and drain):
        if rnd % extra_every == 0 or not (pending or active):
            nx = []
            for g in extra:
                try:
                    next(g)
                    nx.append(g)
                except StopIteration:
                    pass
            extra = nx
        if pending and free and rnd >= next_admit:
            slot = free.pop(0)
            active.append((slot, pending.pop(0)(slot)))
            next_admit = rnd + stagger
        nxt = []
        for slot, g in active:
            try:
                next(g)
                nxt.append((slot, g))
            except StopIteration:
                free.append(slot)
        active = nxt
        rnd += 1


class Builder:
    def __init__(self, depth=DEPTH, debug=False, stop=None):
        self.depth = depth
        self.debug = debug
        self.stop = stop
        self.nc = bass.Bass("TRN2", target_bir_lowering=False)
        self.P = Prog(self.nc)
        self.dbg_outs = []
        self.ring_n = 0

    def din(self, name, shape):
        t = self.nc.dram_tensor(name, list(shape), F32, kind="ExternalInput")
        self.P.reg_dram(t, True)
        return t

    def dout(self, name, shape):
        t = self.nc.dram_tensor(name, list(shape), F32, kind="ExternalOutput")
        self.P.reg_dram(t, False)
        return t

    def dbg(self, name, ap, shape):
        if not self.debug:
            return
        dt = ap.dtype
        t = self.nc.dram_tensor("dbg_" + name, list(shape), dt, kind="ExternalOutput")
        self.P.reg_dram(t, False)
        self.P.dma("sp", t.ap(), ap)
        self.dbg_outs.append("dbg_" + name)

    def chk(self, name):
        if self.stop == name:
            self.P.skip = True

    def carve(self, dtype, shape, nparts=128, at=None):
        n = int(np.prod(shape))
        nbytes = n * _DT_SIZE[dtype]
        if at is None:
            off = (self.ar_off + 31) // 32 * 32
            assert off + nbytes <= self.AR_BYTES, ("arena overflow", off + nbytes)
            self.ar_off = off + nbytes
        else:
            off = at
        self.last_off = off
        base = self.arena if dtype == BF16 else self.arena32
        e0 = off // _DT_SIZE[dtype]
        v = base[0:nparts, e0:e0 + n]
        if len(shape) == 2:
            v = v.rearrange("p (a b) -> p a b", a=shape[0], b=shape[1])
        elif len(shape) == 3:
            v = v.rearrange("p (a b c) -> p a b c", a=shape[0], b=shape[1], c=shape[2])
        return v

    def arena_reset(self):
        self.ar_off = 0

    def ring_load(self, pieces, slot=None):
        if slot is None:
            s = self.ring_n % 2
            self.ring_n += 1
        else:
            s = slot
        slot = self.wr[:, s, :]
        views = []
        for (src, eo, shape) in pieces:
            n = int(np.prod(shape))
            v = self.wr[:, s, eo:eo + n]
            if len(shape) == 2:
                v = v.rearrange("p (a b) -> p a b", a=shape[0], b=shape[1])
            self.P.dma("pool", v, src)
            views.append(v)
        return views

    def wsrc(self, t, layer_off, row0, nk, ncols_total, c0, ncols):
        return bass.AP(t, layer_off + row0 * ncols_total + c0,
                       [[ncols_total, 128], [128 * ncols_total, nk], [1, ncols]])

    def build(self):
        nc, P = self.nc, self.P
        L = self.depth
        xp_d = self.din("xp", [2, SEQ, D_MODEL])
        xs_d = self.din("xs", [DEC_SEQ, D_MODEL])
        cak_d = self.din("cak", [DEPTH, 4, PAST, 128])
        cav_d = self.din("cav", [DEPTH, 4, PAST, 128])
        cbk_d = self.din("cbk", [DEPTH, 4, PAST, 64])
        cbv_d = self.din("cbv", [DEPTH, 4, PAST, 64])
        stc_d = self.din("stc", [DEPTH, 2, 4, 64, 64])
        stn_d = self.din("stn", [DEPTH, 2, 4, 64])
        stm_d = self.din("stm", [DEPTH, 8])
        cvec_d = self.din("cvec", [16, 128])
        w_in_d = self.din("w_in", [DEPTH, D_MODEL, N_IN])
        w_out_d = self.din("w_out", [DEPTH, D_MODEL, D_MODEL])
        ada_w_d = self.din("ada_w", [DEPTH, D_MODEL, 6 * D_MODEL])
        adab_d = self.din("ada_b", [DEPTH * 48, 128])
        lnp_d = self.din("lnp", [128, 128])
        w1_d = self.din("w_mlp1", [DEPTH, D_MODEL, D_FF])
        w2_d = self.din("w_mlp2", [DEPTH, D_FF, D_MODEL])
        gbias_d = self.din("gbias", [DEPTH, 16])
        dlam_d = self.din("dlam", [DEPTH, 256])
        dng_d = self.din("dng", [DEPTH, 128])
        rpb_d = self.din("rpbpad", [DEPTH * 4, 15 * 128])
        mng_d = self.din("mng", [DEPTH, 256])
        c_ident = self.din("ident", [128, 128])
        c_perm = self.din("permR", [128, 128])
        c_ropeC = self.din("ropeC", [128, DEC_SEQ])
        c_ropeS = self.din("ropeS", [128, DEC_SEQ])
        c_maskC = self.din("maskC", [64, 64])
        c_BI = self.din("BI", [2, 8, 260])
        c_SEL = self.din("SEL", [2, 8, 4])
        c_BIJ = self.din("BIJ", [2, 8, 2])
        c_HPSEL = self.din("HPSEL", [8, 128])
        c_maskLT = self.din("maskLT", [2, 64, 260])

        yp_d = self.dout("y_prompt", [2, SEQ, D_MODEL])
        ys_d = self.dout("y_sample", [DEC_SEQ, D_MODEL])
        nak_d = self.dout("new_a_k", [2, DEPTH, 4, SEQ, 128])
        nav_d = self.dout("new_a_v", [2, DEPTH, 4, SEQ, 128])
        nbk_d = self.dout("new_b_k", [2, DEPTH, 4, SEQ, 64])
        nbv_d = self.dout("new_b_v", [2, DEPTH, 4, SEQ, 64])
        ncs_d = self.dout("new_c", [2, DEPTH, 2, 4, 64, 64])
        nns_d = self.dout("new_n", [2, DEPTH, 2, 4, 64])
        nms_d = self.dout("new_m", [2, DEPTH, 8])
        gscr = nc.dram_tensor("gscr", [DEPTH * 4, 64, 1920], F32)
        P.reg_dram(gscr, False)

        ps = P.psum_all()
        psb = P.psb
        self.x = x = P.sbuf("x", [128, 8, NT], F32)
        self.h = h = P.sbuf("h", [128, 8, NT], BF16)
        self.wr = P.sbuf("wr", [128, 2, 8192], BF16)
        self.AR_BYTES = 74 * 1024
        self.arena = P.sbuf("arena", [128, self.AR_BYTES // 2], BF16)
        self.arena32 = self.arena.bitcast(F32)
        ropeC = P.sbuf("ropeC_s", [128, DEC_SEQ], F32)
        ropeS = P.sbuf("ropeS_s", [128, DEC_SEQ], F32)
        ident = P.sbuf("ident_s", [128, 128], F32)
        identb = P.sbuf("identb", [128, 128], BF16)
        permR = P.sbuf("permR_s", [128, 128], F32)
        onesb = P.sbuf("onesb", [128, 128], BF16)
        maskC = P.sbuf("maskC_s", [128, 64], F32)
        BI = P.sbuf("BI_s", [8, 2, 260], F32)
        SEL = P.sbuf("SEL_s", [8, 2, 4], F32)
        BIJ = P.sbuf("BIJ_s", [8, 2, 2], F32)
        HPSEL = P.sbuf("HPSEL_s", [8, 128], F32)
        maskLT = P.sbuf("maskLT_s", [64, 2, 260], F32)
        ones8 = P.sbuf("ones8", [8, 64], F32)
        nBI = P.sbuf("nBI", [8, 2, 4, 65], F32)
        lnp = P.sbuf("lnp_s", [128, 128], F32)
        adab = P.sbuf("adab_s", [128, DEPTH * 48], F32)
        scb = P.sbuf("scb", [128, 8, 2], BF16)
        modvs = [P.sbuf("modv0", [128, 48, 2], F32), P.sbuf("modv1", [128, 48, 2], F32)]
        lam = P.sbuf("lam", [128, 8], F32)
        dl = P.sbuf("dl", [128, 256], F32)
        gAb = P.sbuf("gAb", [128, 128], F32)
        gCb = P.sbuf("gCb", [64, 256], F32)
        gb16 = P.sbuf("gb16", [16, 1], F32)
        st4 = P.sbuf("st4", [128, 1024], F32)
        sm = P.sbuf("sm", [128, 96], F32)
        CN = P.sbuf("CN", [128, 2, 2, 65], F32)
        CNb = P.sbuf("CNb", [128, 2, 2, 65], BF16)
        mprev = P.sbuf("mprev", [8, 20], F32)
        mfin = P.sbuf("mfin", [8, 4], F32)
        LD = P.sbuf("LDs", [8, 16], F32)
        small = P.sbuf("small", [128, 64], F32)

        P.dma("sp", ident[:], c_ident.ap())
        P.dma("sp", permR[:], c_perm.ap())
        P.dma("sp", ropeC[:], c_ropeC.ap())
        P.dma("sp", ropeS[:], c_ropeS.ap())
        P.dma("sp", maskC[0:64, :], c_maskC.ap())
        P.dma("sp", maskC[64:128, :], c_maskC.ap())
        P.dma("sp", BI[:], bass.AP(c_BI, 0, [[260, 8], [8 * 260, 2], [1, 260]]))
        P.dma("sp", SEL[:], bass.AP(c_SEL, 0, [[4, 8], [32, 2], [1, 4]]))
        P.dma("sp", BIJ[:], bass.AP(c_BIJ, 0, [[2, 8], [16, 2], [1, 2]]))
        P.dma("sp", HPSEL[:], c_HPSEL.ap())
        P.dma("sp", maskLT[:], bass.AP(c_maskLT, 0, [[260, 64], [64 * 260, 2], [1, 260]]))
        P.copy(identb[:], ident[:])
        P.ts(nBI[:].rearrange("p d h e -> p (d h e)"), BI[:].rearrange("p d q -> p (d q)"), -1.0, None, ALU.mult)
        P.memset(onesb[:], 1.0)
        P.memset(ones8[:], 1.0)
        P.dma("sp", gscr.ap(), bass.AP(rpb_d, 0, [[1920, DEPTH * 4], [0, 64], [1, 1920]]))
        P.dma("sp", st4[:, 0:128], lnp_d.ap())
        P.transpose(ps[:, 0:128], st4[:, 0:128], ident[:])
        P.copy(lnp[:], ps[:, 0:128])
        for i in range(2):
            r0 = i * 96
            P.dma("sp", st4[0:96, 128:256], adab_d.ap()[r0:r0 + 96, :])
            P.transpose(ps[:, 512:512 + 96], st4[0:96, 128:256], ident[0:96, 0:96])
            P.copy(adab[:, r0:r0 + 96], ps[:, 512:512 + 96])
        P.dma("sp", st4[0:16, 256:384], cvec_d.ap())
        P.transpose(ps[:, 1024:1040], st4[0:16, 256:384], ident[0:16, 0:16])
        P.act(small[:, 0:16], ps[:, 1024:1040], AF.Sigmoid)
        P.tt(scb[:].rearrange("p k c -> p c k"), small[:, 0:16].rearrange("p (c k) -> p c k", c=2, k=8),
             ps[:, 1024:1040].rearrange("p (c k) -> p c k", c=2, k=8), ALU.mult)

        self.arena_reset()
        stgs = [self.carve(F32, (1024,)) for _ in range(6)]
        for blk in range(12):
            if blk < 4:
                src = xp_d.ap()[blk // 2, (blk % 2) * 128:(blk % 2) * 128 + 128, :]
            else:
                src = xs_d.ap()[(blk - 4) * 128:(blk - 4) * 128 + 128, :]
            stg = stgs[blk % 6]
            P.dma("sp", stg, src)
            for half in range(2):
                b = P.bank()
                for c4 in range(4):
                    c = half * 4 + c4
                    P.transpose(ps[:, b * 512 + c4 * 128: b * 512 + c4 * 128 + 128], stg[:, c * 128:(c + 1) * 128], ident[:])
                P.copy(x[:, half * 4:half * 4 + 4, blk * 128:(blk + 1) * 128],
                       ps[:, b * 512:(b + 1) * 512].rearrange("p (c t) -> p c t", c=4, t=128),
                       eng="act" if half else "dve")

        tiles = [(0, 0), (1, 1), (2, 1)]

        for l in range(L):
            lam_init = 0.8 - 0.6 * math.exp(-0.3 * l)
            P.dma("sp", dl[:], bass.AP(dlam_d, l * 256, [[0, 128], [1, 256]]))
            P.tt(dl[:, 0:64], dl[:, 0:64], dl[:, 64:128], ALU.mult)
            P.tt(dl[:, 128:192], dl[:, 128:192], dl[:, 192:256], ALU.mult)
            P.reduce(lam[:, 0:1], dl[:, 0:64], ALU.add)
            P.reduce(lam[:, 1:2], dl[:, 128:192], ALU.add)
            P.act(lam[:, 2:4], lam[:, 0:2], AF.Exp)
            P.tt(lam[:, 4:5], lam[:, 2:3], lam[:, 3:4], ALU.subtract)
            P.ts(lam[:, 5:6], lam[:, 4:5], -1.0, -lam_init, ALU.mult, ALU.add)
            neglam = lam[:, 5:6]
            P.dma("sp", gAb[:], bass.AP(dng_d, l * 128, [[0, 128], [1, 128]]))
            P.ts(gAb[:], gAb[:], 1.0 - lam_init, None, ALU.mult)
            P.dma("sp", gCb[:], bass.AP(mng_d, l * 256, [[0, 64], [1, 256]]))
            P.dma("sp", gb16[:], bass.AP(gbias_d, l * 16, [[1, 16], [1, 1]]))

            self.chk("load")
            modv = modvs[l % 2]

            def ada_gen(la, mv, slot=None):
                for g in range(6):
                    (wv,) = self.ring_load([(self.wsrc(ada_w_d, la * D_MODEL * 6144, 0, 8, 6144, g * 1024, 1024), 0, (8, 1024))], slot=slot)
                    for cc in range(8):
                        j = g * 8 + cc
                        bk, col = (6, 448 + 2 * j) if j < 24 else (7, 448 + 2 * (j - 24))
                        for k in range(8):
                            P.mm(ps[:, bk * 512 + col: bk * 512 + col + 2], wv[:, k, cc * 128:(cc + 1) * 128], scb[:, k, :],
                                 start=(k == 0), stop=(k == 7))
                        yield
                for hf in range(2):
                    bk = 6 + hf
                    P.tt(mv[:, hf * 24:(hf + 1) * 24, :], ps[:, bk * 512 + 448: bk * 512 + 496].rearrange("p (j c) -> p j c", j=24, c=2),
                         adab[:, la * 48 + hf * 24: la * 48 + (hf + 1) * 24].unsqueeze(2).to_broadcast([128, 24, 2]), ALU.add)
                P.ts(mv[:, 8:16, :], mv[:, 8:16, :], 1.0, None, ALU.add)
                P.ts(mv[:, 32:40, :], mv[:, 32:40, :], 1.0, None, ALU.add)
                P.ts(mv[:, 16:24, :], mv[:, 16:24, :], 1.0 / ALPHA, None, ALU.mult)
                P.ts(mv[:, 40:48, :], mv[:, 40:48, :], 1.0 / ALPHA, None, ALU.mult)
                yield

            if l == 0:
                for _ in ada_gen(0, modv):
                    pass

            def modulate(which_sh, which_sc):
                n = 0
                for c in range(8):
                    for (cond, t0, t1) in ((0, 0, 512), (1, 512, NT)):
                        sc_ap = modv[:, which_sc * 8 + c, cond:cond + 1]
                        sh_ap = modv[:, which_sh * 8 + c, cond:cond + 1]
                        if n % 2 == 0:
                            P.ts(h[:, c, t0:t1], x[:, c, t0:t1], sc_ap, sh_ap, ALU.mult, ALU.add)
                        else:
                            P.act(h[:, c, t0:t1], x[:, c, t0:t1], AF.Identity, bias=sh_ap, scale=sc_ap)
                        n += 1

            def wout_partial(wo, nk, mixT, tile_list, gidx):
                for (ti, cond, mcol) in tile_list:
                    for fc in range(8):
                        b = P.bank()
                        for k in range(nk):
                            P.mm(ps[:, b * 512:(b + 1) * 512], wo[:, k, fc * 128:(fc + 1) * 128], mixT[:, k, mcol:mcol + 512],
                                 start=(k == 0), stop=(k == nk - 1))
                        P.stt(x[:, fc, ti * 512:(ti + 1) * 512], ps[:, b * 512:(b + 1) * 512],
                              modv[:, gidx * 8 + fc, cond:cond + 1], x[:, fc, ti * 512:(ti + 1) * 512], ALU.mult, ALU.add)

            def layer_norm(vec_g, vec_b):
                self.arena_reset()
                sqs = [self.carve(BF16, (8, 512)) for _ in range(3)]
                xbs = [self.carve(BF16, (8, 512)) for _ in range(3)]
                means = [self.carve(F32, (512,)) for _ in range(3)]
                rstds = [self.carve(F32, (512,)) for _ in range(3)]
                tmpv = self.carve(F32, (512,))
                for ti in range(3):
                    tsl = slice(ti * 512, (ti + 1) * 512)
                    sq, xb, mean, rstd = sqs[ti], xbs[ti], means[ti], rstds[ti]
                    for c in range(8):
                        if c % 2 == 0:
                            P.copy(xb[:, c, :], x[:, c, tsl])
                        else:
                            P.copy(xb[:, c, :], x[:, c, tsl], eng="act")
                        P.act(sq[:, c, :], x[:, c, tsl], AF.Square)
                    b1 = 2 * ti
                    b2 = 2 * ti + 1
                    for c in range(8):
                        P.mm(ps[:, b1 * 512:(b1 + 1) * 512], onesb[:], xb[:, c, :], start=(c == 0), stop=(c == 7))
                    for c in range(8):
                        P.mm(ps[:, b2 * 512:(b2 + 1) * 512], onesb[:], sq[:, c, :], start=(c == 0), stop=(c == 7))
                for ti in range(3):
                    mean, rstd = means[ti], rstds[ti]
                    b1, b2 = 2 * ti, 2 * ti + 1
                    P.ts(mean, ps[:, b1 * 512:(b1 + 1) * 512], 1.0 / 1024, None, ALU.mult)
                    P.tt(tmpv, mean, mean, ALU.mult)
                    P.stt(rstd, ps[:, b2 * 512:(b2 + 1) * 512], 1.0 / 1024, tmpv, ALU.mult, ALU.subtract)
                    P.ts(rstd, rstd, EPS / (ALPHA * ALPHA), None, ALU.add)
                    P.act(rstd, rstd, AF.Ln)
                    P.act(rstd, rstd, AF.Exp, scale=-0.5)
                for ti in range(3):
                    tsl = slice(ti * 512, (ti + 1) * 512)
                    mean, rstd = means[ti], rstds[ti]
                    for c in range(8):
                        P.tt(x[:, c, tsl], x[:, c, tsl], mean, ALU.subtract)
                        P.tt(x[:, c, tsl], x[:, c, tsl], rstd, ALU.mult)
                        gcol = lnp[:, vec_g * 32 + l * 8 + c: vec_g * 32 + l * 8 + c + 1]
                        bcol = lnp[:, vec_b * 32 + l * 8 + c: vec_b * 32 + l * 8 + c + 1]
                        P.act(x[:, c, tsl], x[:, c, tsl], AF.Identity, bias=bcol, scale=gcol)

            modulate(0, 1)
            if self.debug and l == 0:
                self.dbg("h1", h[:], [128, 8, NT])
            self.chk("mod")

            self.arena_reset()
            qT = self.carve(BF16, (4, NT))
            kT = self.carve(BF16, (4, 2048))
            vA = self.carve(BF16, (16, 512))
            mixT = qT
            eSb = [self.carve(BF16, (2, 1536))]
            es0_off = self.last_off
            eSb.append(self.carve(BF16, (2, 1536)))
            ckTM = self.carve(BF16, (4, 4, 128), at=self.last_off)
            ATb = [self.carve(BF16, (12, 128)), self.carve(BF16, (12, 128))]
            at1 = self.last_off
            ropet = [(self.carve(F32, (512,), at=at1 - 3072), self.carve(F32, (512,), at=at1 - 3072 + 2048)),
                     (self.carve(F32, (512,), at=es0_off), self.carve(F32, (512,), at=es0_off + 2048))]
            rope_n = [0]
            onbb = [self.carve(BF16, (128,)), self.carve(BF16, (128,))]
            junk = self.carve(F32, (128,))
            for ch in range(4):
                P.dma("pool", vA[:, 4 + ch, :].rearrange("p (h e) -> p h e", h=4, e=128),
                      bass.AP(cav_d, l * 4 * PAST * 128 + ch * 128 * 128, [[128, 128], [PAST * 128, 4], [1, 128]]))
                P.dma("pool", ckTM[:, ch, :, :],
                      bass.AP(cak_d, l * 4 * PAST * 128 + ch * 128 * 128, [[128, 128], [PAST * 128, 4], [1, 128]]))
            woff = l * D_MODEL * N_IN
            a1_slot = self.ring_n % 2
            (wA1,) = self.ring_load([(self.wsrc(w_in_d, woff, 0, 8, N_IN, 0, 1024), 0, (8, 1024))])
            wA2, woA = self.ring_load([
                (self.wsrc(w_in_d, woff, 0, 8, N_IN, 1024, 512), 0, (8, 512)),
                (self.wsrc(w_out_d, l * D_MODEL * D_MODEL, 0, 4, D_MODEL, 0, 1024), 4096, (4, 1024))])
            for hd in range(4):
                b = P.bank()
                for ch in range(4):
                    P.transpose(psb[:, b * 1024 + ch * 128: b * 1024 + ch * 128 + 128], ckTM[:, ch, hd, :], identb[:])
                P.copy(kT[:, hd, 512:1024], psb[:, b * 1024: b * 1024 + 512], eng="act")
            for (ti, cond) in tiles:
                for cc in range(8):
                    b = P.bank()
                    for k in range(8):
                        P.mm(ps[:, b * 512:(b + 1) * 512], wA1[:, k, cc * 128:(cc + 1) * 128], h[:, k, ti * 512:(ti + 1) * 512],
                             start=(k == 0), stop=(k == 7))
                    if cc < 4:
                        dst = qT[:, cc, ti * 512:(ti + 1) * 512]
                    else:
                        dst = kT[:, cc - 4, 0:512] if ti == 0 else kT[:, cc - 4, 1024 + (ti - 1) * 512: 1024 + ti * 512]
                    if ti == 0:
                        P.copy(dst, ps[:, b * 512:(b + 1) * 512], eng="act")
                    else:
                        tk = slice((ti - 1) * 512, ti * 512)
                        xs32, rt1 = ropet[rope_n[0] % 2]
                        rope_n[0] += 1
                        P.copy(xs32, ps[:, b * 512:(b + 1) * 512], eng="act")
                        b2 = P.bank()
                        P.mm(ps[:, b2 * 512:(b2 + 1) * 512], permR[:], xs32)
                        P.tt(rt1, xs32, ropeC[:, tk], ALU.mult)
                        P.tt(xs32, ps[:, b2 * 512:(b2 + 1) * 512], ropeS[:, tk], ALU.mult)
                        P.tt(dst, rt1, xs32, ALU.add)
            for blk in range(4):
                b = P.bank()
                for k in range(8):
                    P.mm(ps[:, b * 512:(b + 1) * 512], h[:, k, blk * 128:(blk + 1) * 128], wA1[:, k, 512:1024],
                         start=(k == 0), stop=(k == 7))
                P.copy(st4[:, 0:512], ps[:, b * 512:(b + 1) * 512], eng="act")
                s_, t0 = blk // 2, (blk % 2) * 128
                P.dma("sp", bass.AP(nak_d, ((s_ * DEPTH + l) * 4) * SEQ * 128 + t0 * 128, [[128, 128], [SEQ * 128, 4], [1, 128]]),
                      st4[:, 0:512].rearrange("p (h e) -> p h e", h=4, e=128))
            for blk in range(12):
                b = P.bank()
                for k in range(8):
                    P.mm(ps[:, b * 512:(b + 1) * 512], h[:, k, blk * 128:(blk + 1) * 128], wA2[:, k, :],
                         start=(k == 0), stop=(k == 7))
                vidx = blk if blk < 4 else blk + 4
                P.copy(vA[:, vidx, :], ps[:, b * 512:(b + 1) * 512], eng="act")
                if blk < 4:
                    P.copy(st4[:, 512:1024], ps[:, b * 512:(b + 1) * 512])
                    s_, t0 = blk // 2, (blk % 2) * 128
                    P.dma("sp", bass.AP(nav_d, ((s_ * DEPTH + l) * 4) * SEQ * 128 + t0 * 128, [[128, 128], [SEQ * 128, 4], [1, 128]]),
                          st4[:, 512:1024].rearrange("p (h e) -> p h e", h=4, e=128))

            self.chk("Aproj")

            def attn_A_threads(q0, nqt, k0, nkeys, vblk0, bufs=None, sbank=None, mbank=None):
                scale = 0.125
                nkb = (nkeys + 511) // 512
                nkc = nkeys // 128
                facs = []
                for qt in range(nqt):
                    for hd in range(4):
                        def th(slot, qt=qt, hd=hd):
                            qc = q0 + qt * 128
                            if bufs is None:
                                eS, AT, onb = eSb[slot], ATb[slot], onbb[slot]
                                b = 3 * slot
                            else:
                                eS, AT, onb = bufs[slot]
                                b = sbank(slot)
                            so = 16 * slot
                            st = lambda a, b_: sm[:, so + a:so + b_]
                            for m in range(2):
                                for i in range(nkb):
                                    w = min(512, nkeys - i * 512)
                                    P.mm(ps[:, (b + i) * 512:(b + i) * 512 + w], qT[m * 64:(m + 1) * 64, hd, qc:qc + 128],
                                         kT[m * 64:(m + 1) * 64, hd, k0 + i * 512:k0 + i * 512 + w])
                                yield
                                P.reduce(st(m, m + 1), ps[:, b * 512:b * 512 + nkeys], ALU.max)
                                P.ts(st(2 + m, 3 + m), st(m, m + 1), -scale, None, ALU.mult)
                                P.act(eS[:, m, 0:nkeys], ps[:, b * 512:b * 512 + nkeys], AF.Exp, bias=st(2 + m, 3 + m), scale=scale,
                                      accum_out=st(4 + m, 5 + m))
                                yield
                            P.recip(st(6, 8), st(4, 6))
                            P.tt(st(8, 9), st(7, 8), neglam, ALU.mult)
                            P.ts(eS[:, 0, 0:nkeys], eS[:, 0, 0:nkeys], st(6, 7), None, ALU.mult)
                            P.stt(eS[:, 0, 0:nkeys], eS[:, 1, 0:nkeys], st(8, 9), eS[:, 0, 0:nkeys], ALU.mult, ALU.add)
                            yield
                            bt_ = (6 + slot) if mbank is None else mbank(slot)
                            for g0 in range(0, nkc, 4):
                                gn = min(4, nkc - g0)
                                for kc in range(gn):
                                    P.transpose(psb[:, bt_ * 1024 + kc * 128: bt_ * 1024 + kc * 128 + 128],
                                                eS[:, 0, (g0 + kc) * 128:(g0 + kc + 1) * 128], identb[:])
                                P.copy(AT[:, g0:g0 + gn, :], psb[:, bt_ * 1024: bt_ * 1024 + gn * 128].rearrange("p (a b) -> p a b", a=gn, b=128),
                                       eng="act")
                                yield
                            ops_ = ps[:, bt_ * 512 + 256: bt_ * 512 + 384]
                            for kc in range(nkc):
                                P.mm(ops_, AT[:, kc, :], vA[:, vblk0 + kc, hd * 128:(hd + 1) * 128],
                                     start=(kc == 0), stop=(kc == nkc - 1))
                            yield
                            P.act(junk, ops_, AF.Square, accum_out=st(9, 10))
                            P.ts(st(10, 11), st(9, 10), 1.0 / 128, EPS, ALU.mult, ALU.add)
                            P.act(st(10, 11), st(10, 11), AF.Ln)
                            P.act(st(11, 12), st(10, 11), AF.Exp, scale=-0.5)
                            P.stt(onb, ops_, st(11, 12), gAb[:], ALU.mult, ALU.mult)
                            yield
                            P.transpose(psb[:, bt_ * 1024 + 768: bt_ * 1024 + 896], onb, identb[:])
                            P.copy(mixT[:, hd, qc:qc + 128], psb[:, bt_ * 1024 + 768: bt_ * 1024 + 896], eng="act")
                        facs.append(th)
                return facs

            ada_extra = [ada_gen(l + 1, modvs[(l + 1) % 2], slot=a1_slot)] if l + 1 < L else []
            pbufs = [(self.carve(BF16, (2, 256)), self.carve(BF16, (2, 128)), self.carve(BF16, (128,))) for _ in range(4)]
            run_pipeline(attn_A_threads(0, 2, 0, 256, 0, pbufs, lambda sl: sl, lambda sl: 4 + sl)
                         + attn_A_threads(256, 2, 256, 256, 2, pbufs, lambda sl: sl, lambda sl: 4 + sl), 4, 2,
                         extra=ada_extra, extra_every=6, drain=False)
            run_pipeline(attn_A_threads(512, 8, 512, 1536, 4), 2, 4, extra=ada_extra, extra_every=4)
            if self.debug and l == 0:
                self.dbg("mixA", mixT, [128, 4, NT])
            wout_partial(woA, 4, mixT, [(0, 0, 0), (1, 1, 512), (2, 1, 1024)], 2)

            self.chk("A")
            self.arena_reset()
            qB = self.carve(BF16, (2, NT))
            kB = self.carve(BF16, (2, 2048))
            vB = self.carve(BF16, (16, 256))
            vBs = self.carve(BF16, (8, 256))
            mixB = self.carve(BF16, (2, NT))
            tband = self.carve(BF16, (4, 15, 64))
            tbst = self.carve(F32, (15, 64))
            eBb = [self.carve(BF16, (1024,)) for _ in range(4)]
            eTBb = [self.carve(BF16, (8, 128)) for _ in range(4)]
            obBb = [self.carve(BF16, (256,)) for _ in range(2)]
            cbkTM = self.carve(BF16, (4, 4, 64))
            wB, woB = self.ring_load([
                (self.wsrc(w_in_d, woff, 0, 8, N_IN, 1536, 768), 0, (8, 768)),
                (self.wsrc(w_out_d, l * D_MODEL * D_MODEL, 512, 2, D_MODEL, 0, 1024), 6144, (2, 1024))])
            for ch in range(4):
                P.dma("pool", vB[:, 4 + ch, :].rearrange("p (h e) -> p h e", h=4, e=64),
                      bass.AP(cbv_d, l * 4 * PAST * 64 + ch * 128 * 64, [[64, 128], [PAST * 64, 4], [1, 64]]))
                P.dma("pool", cbkTM[:, ch, :, :],
                      bass.AP(cbk_d, l * 4 * PAST * 64 + ch * 128 * 64, [[64, 128], [PAST * 64, 4], [1, 64]]))
            for j in range(2):
                b = P.bank()
                for ch in range(4):
                    P.transpose(psb[:, b * 1024 + ch * 128: b * 1024 + ch * 128 + 128],
                                cbkTM[:, ch, 2 * j:2 * j + 2, :], identb[:])
                P.copy(kB[:, j, 512:1024], psb[:, b * 1024: b * 1024 + 512], eng="act")
            for hd in range(4):
                for half in range(2):
                    P.dma("sp", tbst[half * 64:(half + 1) * 64], bass.AP(gscr, ((l * 4 + hd) * 64) * 1920 + 63, [[1919, 64], [128, 15], [1, 64]]))
                P.tt(tbst, tbst, maskC[:].unsqueeze(1).to_broadcast([128, 15, 64]), ALU.add)
                P.ts(tband[:, hd, :, :], tbst, 8.0, None, ALU.mult)
            for (ti, cond) in tiles:
                for cc in range(4):
                    b = P.bank()
                    for k in range(8):
                        P.mm(ps[:, b * 512:(b + 1) * 512], wB[:, k, cc * 128:(cc + 1) * 128], h[:, k, ti * 512:(ti + 1) * 512],
                             start=(k == 0), stop=(k == 7))
                    if cc < 2:
                        dst = qB[:, cc, ti * 512:(ti + 1) * 512]
                    else:
                        dst = kB[:, cc - 2, 0:512] if ti == 0 else kB[:, cc - 2, 1024 + (ti - 1) * 512: 1024 + ti * 512]
                    P.copy(dst, ps[:, b * 512:(b + 1) * 512], eng="act")
            for blk in range(12):
                b = P.bank()
                for k in range(8):
                    P.mm(ps[:, b * 512: b * 512 + 512], h[:, k, blk * 128:(blk + 1) * 128], wB[:, k, 256:768],
                         start=(k == 0), stop=(k == 7))
                vidx = blk if blk < 4 else blk + 4
                P.copy(vB[:, vidx, :], ps[:, b * 512 + 256: b * 512 + 512], eng="act")
                if blk < 4:
                    P.copy(st4[:, 0:512], ps[:, b * 512: b * 512 + 512])
                    s_, t0 = blk // 2, (blk % 2) * 128
                    P.dma("sp", bass.AP(nbk_d, ((s_ * DEPTH + l) * 4) * SEQ * 64 + t0 * 64, [[64, 128], [SEQ * 64, 4], [1, 64]]),
                          st4[:, 0:256].rearrange("p (h e) -> p h e", h=4, e=64))
                    P.dma("sp", bass.AP(nbv_d, ((s_ * DEPTH + l) * 4) * SEQ * 64 + t0 * 64, [[64, 128], [SEQ * 64, 4], [1, 64]]),
                          st4[:, 256:512].rearrange("p (h e) -> p h e", h=4, e=64))
            for jb in range(7):
                b = P.bank()
                t0 = 512 + 64 + jb * 128
                for k in range(8):
                    P.mm(ps[:, b * 512: b * 512 + 256], h[:, k, t0:t0 + 128], wB[:, k, 512:768], start=(k == 0), stop=(k == 7))
                P.copy(vBs[:, jb, :], ps[:, b * 512: b * 512 + 256], eng="act")
            scaleB = 0.125

            def bp_threads():
                facs = []
                for s_ in range(2):
                    for qt in range(2):
                        for hd in range(4):
                            def th(slot, s_=s_, qt=qt, hd=hd):
                                qc = s_ * 256 + qt * 128
                                j, hp = hd // 2, hd % 2
                                eB, eTB, obB = eBb[slot], eTBb[slot], obBb[(s_ * 2 + qt) % 2]
                                so = 32 + 10 * slot
                                st = lambda a, b_: sm[:, so + a:so + b_]
                                b = 2 * slot
                                bt = 2 * slot + 1
                                P.mm(ps[:, b * 512: b * 512 + 256], qB[hp * 64:(hp + 1) * 64, j, qc:qc + 128],
                                     kB[hp * 64:(hp + 1) * 64, j, s_ * 256:(s_ + 1) * 256])
                                yield
                                P.reduce(st(0, 1), ps[:, b * 512: b * 512 + 256], ALU.max)
                                P.ts(st(1, 2), st(0, 1), -scaleB, None, ALU.mult)
                                P.act(eB[:, 0:256], ps[:, b * 512: b * 512 + 256], AF.Exp, bias=st(1, 2), scale=scaleB, accum_out=st(2, 3))
                                P.recip(st(3, 4), st(2, 3))
                                yield
                                for kc in range(2):
                                    P.transpose(psb[:, bt * 1024 + kc * 128: bt * 1024 + kc * 128 + 128], eB[:, kc * 128:(kc + 1) * 128], identb[:])
                                P.copy(eTB[:, 0:2, :], psb[:, bt * 1024: bt * 1024 + 256].rearrange("p (a b) -> p a b", a=2, b=128), eng="act")
                                yield
                                ops_ = ps[:, bt * 512 + 256: bt * 512 + 320]
                                for kc in range(2):
                                    P.mm(ops_, eTB[:, kc, :], vB[:, 2 * s_ + kc, hd * 64:(hd + 1) * 64], start=(kc == 0), stop=(kc == 1))
                                yield
                                P.ts(obB[:, hd * 64:(hd + 1) * 64], ops_, st(3, 4), None, ALU.mult)
                                yield
                                if hd == 3:
                                    for jj in range(2):
                                        P.transpose(psb[:, bt * 1024 + 768 + jj * 128: bt * 1024 + 896 + jj * 128], obB[:, jj * 128:(jj + 1) * 128], identb[:])
                                    P.copy(mixB[:, :, qc:qc + 128], psb[:, bt * 1024 + 768: bt * 1024 + 1024].rearrange("p (a b) -> p a b", a=2, b=128), eng="act")
                            facs.append(th)
                return facs

            def bs_threads():
                facs = []
                for r in range(16):
                    for hd in range(4):
                        def th(slot, r=r, hd=hd):
                            ks = min(max(r - 4, 0), 8)
                            dr0 = ks - r + 7
                            qc = 512 + r * 64
                            j, hp = hd // 2, hd % 2
                            eB, eTB, obB = eBb[slot], eTBb[slot], obBb[r % 2]
                            so = 32 + 10 * slot
                            st = lambda a, b_: sm[0:64, so + a:so + b_]
                            b = 2 * slot
                            bt = 2 * slot
                            P.mm(ps[0:64, b * 512: b * 512 + 512], qB[hp * 64:(hp + 1) * 64, j, qc:qc + 64],
                                 kB[hp * 64:(hp + 1) * 64, j, 1024 + ks * 64: 1024 + ks * 64 + 512], start=True, stop=False)
                            P.mm(ps[0:64, b * 512: b * 512 + 512], identb[hp * 64:(hp + 1) * 64, hp * 64:(hp + 1) * 64],
                                 tband[hp * 64:(hp + 1) * 64, hd, dr0:dr0 + 8, :].rearrange("p r c -> p (r c)"), start=False, stop=True)
                            P.mm(ps[0:64, (b + 1) * 512: (b + 1) * 512 + 512], qB[hp * 64:(hp + 1) * 64, j, qc:qc + 64],
                                 kB[hp * 64:(hp + 1) * 64, j, 512:1024])
                            yield
                            P.reduce(st(0, 1), ps[0:64, b * 512: b * 512 + 1024], ALU.max)
                            P.ts(st(3, 4), st(0, 1), -scaleB, None, ALU.mult)
                            yield
                            P.act(eB[0:64, 0:1024], ps[0:64, b * 512: b * 512 + 1024], AF.Exp, bias=st(3, 4), scale=scaleB, accum_out=st(6, 7))
                            P.recip(st(7, 8), st(6, 7))
                            yield
                            for kc in range(8):
                                P.transpose(psb[:, bt * 1024 + kc * 64: bt * 1024 + kc * 64 + 64], eB[0:64, kc * 128:(kc + 1) * 128],
                                            identb[0:64, 0:64])
                            P.copy(eTB[:, :, 0:64], psb[:, bt * 1024: bt * 1024 + 512].rearrange("p (a b) -> p a b", a=8, b=64), eng="act")
                            yield
                            ops_ = ps[0:64, bt * 512 + 256: bt * 512 + 320]
                            for kc in range(8):
                                if kc < 4:
                                    if ks % 2 == 0:
                                        vv = vB[:, 8 + ks // 2 + kc, hd * 64:(hd + 1) * 64]
                                    else:
                                        vv = vBs[:, (ks - 1) // 2 + kc, hd * 64:(hd + 1) * 64]
                                else:
                                    vv = vB[:, kc, hd * 64:(hd + 1) * 64]
                                P.mm(ops_, eTB[:, kc, 0:64], vv, start=(kc == 0), stop=(kc == 7))
                            yield
                            P.ts(obB[0:64, hd * 64:(hd + 1) * 64], ops_, st(7, 8), None, ALU.mult)
                            yield
                            if hd == 3:
                                for jj in range(2):
                                    P.transpose(psb[:, bt * 1024 + 768 + jj * 64: bt * 1024 + 832 + jj * 64], obB[0:64, jj * 128:(jj + 1) * 128],
                                                identb[0:64, 0:64])
                                P.copy(mixB[:, :, qc:qc + 64], psb[:, bt * 1024 + 768: bt * 1024 + 896].rearrange("p (a b) -> p a b", a=2, b=64), eng="act")
                        facs.append(th)
                return facs

            run_pipeline(bp_threads() + bs_threads(), 4, 2)
            if self.debug and l == 0:
                self.dbg("mixB", mixB, [128, 2, NT])
            wout_partial(woB, 2, mixB, [(0, 0, 0), (1, 1, 512), (2, 1, 1024)], 2)

            self.chk("B")
            wC1, wCg = self.ring_load([
                (self.wsrc(w_in_d, woff, 0, 8, N_IN, 2304, 512), 0, (8, 512)),
                (self.wsrc(w_in_d, woff, 0, 8, N_IN, 3328, 16), 4096, (8, 16))])
            wC2, woC = self.ring_load([
                (self.wsrc(w_in_d, woff, 0, 8, N_IN, 2816, 512), 0, (8, 512)),
                (self.wsrc(w_out_d, l * D_MODEL * D_MODEL, 768, 2, D_MODEL, 0, 1024), 4096, (2, 1024))])
            for grp in range(2):
                self.arena_reset()
                if grp == 0:
                    T, NCH, g0, gtiles, resets = 512, 8, 0, [(0, 0)], (0, 4)
                else:
                    T, NCH, g0, gtiles, resets = 1024, 16, 512, [(1, 1), (2, 1)], (0,)
                qC = self.carve(BF16, (2, T))
                kC = self.carve(BF16, (2, T))
                kTM = self.carve(BF16, (NCH, 256), nparts=64)
                V1 = self.carve(BF16, (NCH, 4, 65), nparts=64)
                sigO = self.carve(BF16, (NCH, 256), nparts=64)
                hfw = self.carve(BF16, (NCH, 256), nparts=64)
                mixC = qC
                T1 = self.carve(F32, (T,), nparts=16)
                T2 = self.carve(F32, (T,), nparts=16)
                T3 = self.carve(F32, (T,), nparts=16)
                I8 = self.carve(F32, (T,), nparts=8)
                F8 = self.carve(F32, (T,), nparts=8)
                scr = []
                for _t in range(2):
                    scr.append(dict(
                        NMbd=self.carve(F32, (4, 65), nparts=8), DEC=self.carve(F32, (2,), nparts=8),
                        Dx=self.carve(F32, (4, 65), nparts=64), SD=self.carve(BF16, (4, 64), nparts=64),
                        WV=self.carve(BF16, (4, 65), nparts=64), EN=self.carve(F32, (8,), nparts=64),
                        tmpc=self.carve(F32, (4, 65), nparts=64), hs=self.carve(F32, (4, 64), nparts=64),
                        o3=self.carve(BF16, (256,), nparts=64), cst=self.carve(F32, (16,), nparts=64),
                        dcy=self.carve(F32, (2,)), t1c=self.carve(F32, (2, 65))))
                tmpc = scr[0]["tmpc"]
                sgt = tmpc.rearrange("p h e -> p (h e)")[:, 0:256]
                P.memset(V1[:, :, :, 64:65], 1.0)
                for (ti, cond) in gtiles:
                    lt = ti * 512 - g0
                    for cc in range(4):
                        b = P.bank()
                        for k in range(8):
                            P.mm(ps[:, b * 512:(b + 1) * 512], wC1[:, k, cc * 128:(cc + 1) * 128], h[:, k, ti * 512:(ti + 1) * 512],
                                 start=(k == 0), stop=(k == 7))
                        dst = qC[:, cc, lt:lt + 512] if cc < 2 else kC[:, cc - 2, lt:lt + 512]
                        P.copy(dst, ps[:, b * 512:(b + 1) * 512], eng="act")
                    b = P.bank()
                    for k in range(8):
                        P.mm(ps[0:16, b * 512:(b + 1) * 512], wCg[:, k, :], h[:, k, ti * 512:(ti + 1) * 512], start=(k == 0), stop=(k == 7))
                    P.ts(T1[:, lt:lt + 512], ps[0:16, b * 512:(b + 1) * 512], gb16[:, 0:1], None, ALU.add)
                for c in range(NCH):
                    t0 = g0 + c * 64
                    b = P.bank()
                    for k in range(8):
                        P.mm(ps[0:64, b * 512: b * 512 + 256], h[:, k, t0:t0 + 64], wC1[:, k, 256:512], start=(k == 0), stop=(k == 7))
                    P.copy(kTM[:, c, :], ps[0:64, b * 512: b * 512 + 256], eng="act")
                    b = P.bank()
                    for k in range(8):
                        P.mm(ps[0:64, b * 512:(b + 1) * 512], h[:, k, t0:t0 + 64], wC2[:, k, :], start=(k == 0), stop=(k == 7))
                    P.copy(V1[:, c, :, 0:64], ps[0:64, b * 512: b * 512 + 256].rearrange("p (h e) -> p h e", h=4, e=64))
                    P.act(sgt, ps[0:64, b * 512 + 256:(b + 1) * 512], AF.Exp, scale=-1.0)
                    P.ts(sgt, sgt, 1.0, None, ALU.add)
                    P.recip(sigO[:, c, :], sgt, lowp=True)
                    P.tt(sigO[:, c, :], sigO[:, c, :], gCb[:], ALU.mult, eng="pool")
                self.chk("C%dproj" % grp)
                P.act(T2, T1, AF.Abs)
                P.act(T2, T2, AF.Exp, scale=-1.0)
                P.act(T2, T2, AF.Ln, bias=1.0)
                P.stt(T2, T1, 0.0, T2, ALU.min, ALU.subtract)
                P.copy(T3, T1[:, ::-1])
                P.dma("sp", I8[0:4, :], T1[0:4, :])
                P.dma("sp", I8[4:8, :], T3[8:12, :])
                P.dma("sp", F8[0:4, :], T2[4:8, :])
                P.copy(T3, T2[:, ::-1])
                P.dma("sp", F8[4:8, :], T3[12:16, :])
                GC = T1[0:8, :]
                BC = T2[0:8, :]
                Mr = T3[0:8, :]
                P.scan(GC, ones8[:, 0:1].to_broadcast([8, T]), F8, 0.0, ALU.mult, ALU.add)
                P.copy(BC[:, 0:64], GC[:, 0:64])
                P.tt(BC[:, 64:T].rearrange("p (c u) -> p c u", c=NCH - 1, u=64),
                     GC[:, 64:T].rearrange("p (c u) -> p c u", c=NCH - 1, u=64),
                     GC[:, 63:T - 64:64].unsqueeze(2).to_broadcast([8, NCH - 1, 64]), ALU.subtract)
                P.tt(I8, I8, BC, ALU.subtract)
                if grp == 0:
                    P.memset(mprev[:, 0:NCH + 1], 0.0)
                else:
                    P.dma("sp", mprev[:, 0:1], bass.AP(stm_d, l * 8, [[1, 8], [1, 1]]))
                for c in range(NCH):
                    cs = slice(c * 64, (c + 1) * 64)
                    P.scan(Mr[:, cs], I8[:, cs], I8[:, cs], mprev[:, c:c + 1], ALU.max, ALU.max)
                    last = (c + 1 == NCH) or ((c + 1) in resets)
                    if last:
                        if grp == 0:
                            P.tt(mfin[:, (c // 4):(c // 4) + 1], BC[:, c * 64 + 63:c * 64 + 64], Mr[:, c * 64 + 63:c * 64 + 64], ALU.add)
                    else:
                        P.tt(mprev[:, c + 1:c + 2], BC[:, c * 64 + 63:c * 64 + 64], Mr[:, c * 64 + 63:c * 64 + 64], ALU.add)
                P.tt(F8.rearrange("p (c u) -> p c u", c=NCH, u=64), mprev[:, 0:NCH].unsqueeze(2).to_broadcast([8, NCH, 64]),
                     Mr.rearrange("p (c u) -> p c u", c=NCH, u=64), ALU.subtract)
                P.stt(GC, BC, -1.0, Mr, ALU.mult, ALU.subtract)
                P.tt(LD[:, 0:NCH], mprev[:, 0:NCH], Mr[:, 63:T:64], ALU.subtract)
                Er, Nr = F8, GC
                for X in (I8, Mr, Er, Nr):
                    P.copy(T2[0:8, :], X[:, ::-1])
                    P.dma("sp", X[4:8, :], T2[4:8, :])
                if grp == 0:
                    for s_ in range(2):
                        P.dma("sp", bass.AP(nms_d, (s_ * DEPTH + l) * 8, [[1, 4], [1, 1]]), mfin[0:4, s_:s_ + 1])
                        P.dma("sp", bass.AP(nms_d, (s_ * DEPTH + l) * 8 + 4, [[1, 4], [1, 1]]), mfin[4:8, 1 - s_:2 - s_])
                self.chk("C%dgate" % grp)
                nseq = 2 if grp == 0 else 1
                cps = NCH // nseq
                hst = hfw
                gsig = sigO

                def chain(slot, d, s_):
                    S_ = scr[slot]
                    NMbd, DEC, Dx, SD, WV, EN = S_["NMbd"], S_["DEC"], S_["Dx"], S_["SD"], S_["WV"], S_["EN"]
                    tmpc_, hs, o3, cst, dcy, t1c = S_["tmpc"], S_["hs"], S_["o3"], S_["cst"], S_["dcy"], S_["t1c"]
                    b0, b1, b2, b3 = 4 * slot, 4 * slot + 1, 4 * slot + 2, 4 * slot + 3
                    bpar = (b1, b2)
                    if grp == 0:
                        P.memset(CN[:, d, :, :], 0.0)
                    else:
                        P.dma("sp", CN[:, d, :, 0:64], bass.AP(stc_d, (l * 2 + d) * 16384, [[64, 128], [8192, 2], [1, 64]]))
                        P.dma("sp", CN[:, d, :, 64:65], bass.AP(stn_d, (l * 2 + d) * 256, [[1, 128], [128, 2], [1, 1]]))
                    P.copy(CNb[:, d, :, :], CN[:, d, :, :])
                    yield
                    order = range(cps) if d == 0 else range(cps - 1, -1, -1)
                    for ci, cl in enumerate(order):
                        c = s_ * cps + cl
                        cf = c if d == 0 else NCH - 1 - c
                        cs = slice(c * 64, (c + 1) * 64)
                        tok = cs
                        endcol = c * 64 + 63 if d == 0 else c * 64
                        first_half = ci < cps // 2
                        lastc = (ci == cps - 1)
                        do_update = not (lastc and grp == 1)
                        BId = BI[:, d, :].rearrange("p (h e) -> p h e", h=4, e=65)
                        P.tt(NMbd[:, :, 0:64], Mr[:, cs].unsqueeze(1).to_broadcast([8, 4, 64]), nBI[:, d, :, 0:64], ALU.mult)
                        P.tt(NMbd[:, :, 64:65], Mr[:, endcol:endcol + 1].unsqueeze(1).to_broadcast([8, 4, 1]), nBI[:, d, :, 64:65], ALU.mult)
                        zps = ps[0:64, b0 * 512: b0 * 512 + 260]
                        P.mm(zps, I8[:, cs], BI[:, d, :], start=True, stop=False)
                        P.mm(zps, ones8[:, :], NMbd.rearrange("p h e -> p (h e)"), start=False, stop=False)
                        P.mm(zps, ident[0:64, 0:64], maskLT[:, d, :], start=False, stop=True)
                        for hp in range(2):
                            for j in range(2):
                                hd = 2 * j + hp
                                P.mm(ps[0:64, bpar[hp] * 512 + hd * 64: bpar[hp] * 512 + hd * 64 + 64], kC[hp * 64:(hp + 1) * 64, j, tok],
                                     qC[hp * 64:(hp + 1) * 64, j, tok])
                        P.mm(ps[0:64, b3 * 512: b3 * 512 + 4], Er[:, cs], SEL[:, d, :])
                        P.mm(ps[0:64, b3 * 512 + 4: b3 * 512 + 8], Nr[:, cs], SEL[:, d, :])
                        yield
                        P.act(Dx.rearrange("p h e -> p (h e)"), zps, AF.Exp)
                        P.act(EN, ps[0:64, b3 * 512: b3 * 512 + 8], AF.Exp)
                        if do_update:
                            P.ts(DEC, BIJ[:, d, :], LD[:, cf:cf + 1], None, ALU.mult)
                            P.mm(ps[:, b0 * 512 + 384: b0 * 512 + 386], HPSEL[:, :], DEC)
                        yield
                        for hp in range(2):
                            P.stt(SD[:, hp::2, :], ps[0:64, bpar[hp] * 512: bpar[hp] * 512 + 256].rearrange("p (h e) -> p h e", h=4, e=64)[:, hp::2, :],
                                  0.125, Dx[:, hp::2, 0:64], ALU.mult, ALU.mult)
                        for hd in range(4):
                            P.act(WV[:, hd, :], V1[:, c, hd, :], AF.Copy, scale=Dx[:, hd, 64:65])
                        if do_update:
                            P.act(dcy, ps[:, b0 * 512 + 384: b0 * 512 + 386], AF.Exp)
                        yield
                        for hp in range(2):
                            for j in range(2):
                                hd = 2 * j + hp
                                P.mm(ps[0:64, bpar[hp] * 512 + hd * 65: bpar[hp] * 512 + hd * 65 + 65], qC[hp * 64:(hp + 1) * 64, j, tok],
                                     CNb[hp * 64:(hp + 1) * 64, d, j, :])
                        for hd in range(4):
                            P.mm(ps[0:64, b3 * 512 + hd * 65: b3 * 512 + hd * 65 + 65], SD[:, hd, :], V1[:, c, hd, :])
                        yield
                        for hp in range(2):
                            P.stt(tmpc_[:, hp::2, :], ps[0:64, bpar[hp] * 512: bpar[hp] * 512 + 260].rearrange("p (h e) -> p h e", h=4, e=65)[:, hp::2, :],
                                  0.125, EN[:, hp:4:2].unsqueeze(2).to_broadcast([64, 2, 65]), ALU.mult, ALU.mult)
                        P.tt(tmpc_, tmpc_, ps[0:64, b3 * 512: b3 * 512 + 260].rearrange("p (h e) -> p h e", h=4, e=65), ALU.add)
                        P.act(cst[:, 0:4], tmpc_[:, :, 64:65].rearrange("p h e -> p (h e)"), AF.Abs)
                        yield
                        P.tt(cst[:, 0:4], cst[:, 0:4], EN[:, 4:8], ALU.max)
                        P.recip(cst[:, 4:8], cst[:, 0:4])
                        if do_update:
                            for j in range(2):
                                P.mm(ps[:, b3 * 512 + j * 130: b3 * 512 + j * 130 + 130], kTM[:, c, j * 128:(j + 1) * 128],
                                     WV[:, 2 * j:2 * j + 2, :].rearrange("p h e -> p (h e)"))
                            P.tt(t1c, CN[:, d, :, :], dcy.unsqueeze(2).to_broadcast([128, 2, 65]), ALU.mult)
                        yield
                        if first_half:
                            for hd in range(4):
                                P.act(hst[:, c, hd * 64:(hd + 1) * 64], tmpc_[:, hd, 0:64], AF.Copy, scale=cst[:, 4 + hd:5 + hd])
                        else:
                            P.tt(hs, tmpc_[:, :, 0:64], cst[:, 4:8].unsqueeze(2).to_broadcast([64, 4, 64]), ALU.mult)
                            P.tt(hs, hs, hst[:, c, :].rearrange("p (h e) -> p h e", h=4, e=64), ALU.add)
                            hs2 = Dx[:, :, 0:64]
                            for hd in range(4):
                                P.act(hs2[:, hd, :], hs[:, hd, :], AF.Square, accum_out=cst[:, 8 + hd:9 + hd])
                            P.ts(cst[:, 8:12], cst[:, 8:12], 1.0 / 64, EPS, ALU.mult, ALU.add)
                            P.act(cst[:, 8:12], cst[:, 8:12], AF.Ln)
                            P.act(cst[:, 12:16], cst[:, 8:12], AF.Exp, scale=-0.5)
                        if do_update:
                            for hp in range(2):
                                pv = ps[hp * 64:(hp + 1) * 64, b3 * 512: b3 * 512 + 260].rearrange("p (j q) -> p j q", j=2, q=130)
                                P.tt(CN[hp * 64:(hp + 1) * 64, d, :, :], t1c[hp * 64:(hp + 1) * 64, :, :],
                                     pv[:, :, hp * 65:hp * 65 + 65], ALU.add)
                            P.copy(CNb[:, d, :, :], CN[:, d, :, :], eng="act")
                        yield
                        if not first_half:
                            P.tt(hs, hs, cst[:, 12:16].unsqueeze(2).to_broadcast([64, 4, 64]), ALU.mult)
                            P.tt(o3.rearrange("p (h e) -> p h e", h=4, e=64), hs, gsig[:, c, :].rearrange("p (h e) -> p h e", h=4, e=64), ALU.mult)
                            for j in range(2):
                                P.transpose(psb[:, b0 * 1024 + 832 + j * 64: b0 * 1024 + 896 + j * 64], o3[:, j * 128:(j + 1) * 128], identb[0:64, 0:64])
                            P.copy(mixC[:, :, tok], psb[:, b0 * 1024 + 832: b0 * 1024 + 960].rearrange("p (a b) -> p a b", a=2, b=64), eng="act")
                        yield
                    if grp == 0:
                        P.dma("sp", bass.AP(ncs_d, ((s_ * DEPTH + l) * 2 + d) * 16384, [[64, 128], [8192, 2], [1, 64]]), CN[:, d, :, 0:64])
                        P.dma("sp", bass.AP(nns_d, ((s_ * DEPTH + l) * 2 + d) * 256, [[1, 128], [128, 2], [1, 1]]), CN[:, d, :, 64:65])

                for s_ in range(nseq):
                    run_pipeline([lambda slot, s_=s_: chain(slot, 0, s_), lambda slot, s_=s_: chain(slot, 1, s_)], 2, 4)
                if self.debug and l == 0:
                    self.dbg("mixC%d" % grp, mixC, [128, 2, T])
                wout_partial(woC, 2, mixC, [(ti, cond, ti * 512 - g0) for (ti, cond) in gtiles], 2)
                if grp == 0:
                    self.chk("C0")

            self.chk("C")
            layer_norm(0, 1)
            if self.debug and l == 0:
                self.dbg("x1", x[:], [128, 8, NT])
            self.chk("ln1")
            modulate(3, 4)
            for e8 in range(8):
                w1v, w2v = self.ring_load([
                    (self.wsrc(w1_d, l * D_MODEL * D_FF, 0, 8, D_FF, e8 * 512, 512), 0, (8, 512)),
                    (self.wsrc(w2_d, l * D_FF * D_MODEL, e8 * 512, 4, D_MODEL, 0, 1024), 4096, (4, 1024))])
                self.arena_reset()
                ub = [self.carve(BF16, (4, 512)) for _ in range(2)]
                rl = [self.carve(BF16, (512,)) for _ in range(2)]
                for (ti, cond) in tiles:
                    u = ub[ti % 2]
                    for fc in range(4):
                        b = P.bank()
                        for k in range(8):
                            P.mm(ps[:, b * 512:(b + 1) * 512], w1v[:, k, fc * 128:(fc + 1) * 128], h[:, k, ti * 512:(ti + 1) * 512],
                                 start=(k == 0), stop=(k == 7))
                        r_ = rl[fc % 2]
                        P.act(r_, ps[:, b * 512:(b + 1) * 512], AF.Relu)
                        P.tt(u[:, fc, :], r_, r_, ALU.mult)
                    for fc in range(8):
                        b = P.bank()
                        for k in range(4):
                            P.mm(ps[:, b * 512:(b + 1) * 512], w2v[:, k, fc * 128:(fc + 1) * 128], u[:, k, :], start=(k == 0), stop=(k == 3))
                        P.stt(x[:, fc, ti * 512:(ti + 1) * 512], ps[:, b * 512:(b + 1) * 512],
                              modv[:, 5 * 8 + fc, cond:cond + 1], x[:, fc, ti * 512:(ti + 1) * 512], ALU.mult, ALU.add)
            layer_norm(2, 3)

        P.skip = False
        self.arena_reset()
        stgs = [self.carve(F32, (1024,)) for _ in range(6)]
        for blk in range(12):
            stg = stgs[blk % 6]
            for half in range(2):
                b = P.bank()
                for c4 in range(4):
                    c = half * 4 + c4
                    P.transpose(ps[:, b * 512 + c4 * 128: b * 512 + c4 * 128 + 128], x[:, c, blk * 128:(blk + 1) * 128], ident[:])
                P.copy(stg[:, half * 512:(half + 1) * 512], ps[:, b * 512:(b + 1) * 512], eng="act" if half else "dve")
            if blk < 4:
                dst = yp_d.ap()[blk // 2, (blk % 2) * 128:(blk % 2) * 128 + 128, :]
            else:
                dst = ys_d.ap()[(blk - 4) * 128:(blk - 4) * 128 + 128, :]
            P.dma("sp", dst, stg)
        P.emit()
        P.close()
        return nc


_PROG_CACHE = {}


def _get_prog(depth=DEPTH, debug=False, stop=None):
    key = (depth, debug, stop)
    if key not in _PROG_CACHE:
        b = Builder(depth, debug, stop)
        nc = b.build()
        _PROG_CACHE[key] = (nc, b.dbg_outs)
    return _PROG_CACHE[key]


def _in_maps(inp):
    global _CONSTS
    if _CONSTS is None:
        _CONSTS = _const_tables()
    f = lambda a: np.ascontiguousarray(np.asarray(a, dtype=np.float32))
    lnp = np.concatenate([f(inp[k]).reshape(32, 128) for k in ("ln1_g", "ln1_b", "ln2_g", "ln2_b")], 0)
    rpb = f(inp["nat_rpb"])
    rpbpad = np.zeros((DEPTH, 4, 15, 128), np.float32)
    rpbpad[..., 48:79] = rpb
    shared = {
        "w_in": f(inp["w_in"]), "w_out": f(inp["w_out"]), "ada_w": f(inp["ada_w"]),
        "ada_b": f(inp["ada_b"]).reshape(DEPTH * 48, 128), "lnp": lnp,
        "w_mlp1": f(inp["w_mlp1"]), "w_mlp2": f(inp["w_mlp2"]),
        "gbias": f(inp["mlstm_gate_bias"]), "dlam": f(inp["diff_lambda"]).reshape(DEPTH, 256),
        "dng": f(inp["diff_norm_g"]), "rpbpad": rpbpad.reshape(DEPTH * 4, 15 * 128),
        "mng": f(inp["mlstm_norm_g"]).reshape(DEPTH, 256),
    }
    shared.update({k: v for k, v in _CONSTS.items()})
    xp = f(inp["x_prompt"]); xs = f(inp["x_sample"])
    maps = []
    for c in range(8):
        b = c // 4
        m = dict(shared)
        m["xp"] = xp[2 * c:2 * c + 2]
        m["xs"] = xs[b]
        m["cak"] = f(inp["cache_a_k"])[b]
        m["cav"] = f(inp["cache_a_v"])[b]
        m["cbk"] = f(inp["cache_b_k"])[b]
        m["cbv"] = f(inp["cache_b_v"])[b]
        m["stc"] = f(inp["state_c"])[b]
        m["stn"] = f(inp["state_n"])[b]
        m["stm"] = f(inp["state_m"])[b].reshape(DEPTH, 8)
        m["cvec"] = np.concatenate([f(inp["c_ctx"]).reshape(8, 128), f(inp["c"])[b].reshape(8, 128)], 0)
        maps.append(m)
    return maps


def kernel(**inputs):
    nc, _ = _get_prog()
    maps = _in_maps(inputs)
    res = run_bass_kernel_spmd(nc, maps, core_ids=list(range(8)))
    r = res.results
    y_prompt = np.concatenate([r[c]["y_prompt"] for c in range(8)], 0)
    y_sample = np.stack([r[0]["y_sample"], r[4]["y_sample"]], 0)
    cat = lambda k: np.concatenate([r[c][k] for c in range(8)], 0)
    new_m = cat("new_m").reshape(BATCH, DEPTH, 2, 4)
    return (y_prompt, y_sample, cat("new_a_k"), cat("new_a_v"), cat("new_b_k"), cat("new_b_v"),
            cat("new_c"), cat("new_n"), new_m)
```
